# Optimizing a Trainium2 kernel written in Bass

```python
import jax, jax.numpy as jnp
from jax import lax
import numpy as np

D_MODEL = 2048
BATCH = 8
SEQ = 2048
DEPTH = 2
DEC_BATCH = 128
DEC_SEQ = 4
PAST_LEN = 8192
PAGE_SIZE = 128

HEAD_DIM = 64
ATT_WIDTH = D_MODEL // 2
ATT_HEADS = ATT_WIDTH // HEAD_DIM
KV_HEADS = ATT_HEADS // 8
GROUP = ATT_HEADS // KV_HEADS
KV_WIDTH = KV_HEADS * HEAD_DIM
WINDOW = 128
BLOCK = 128
SGU_WIDTH = D_MODEL - ATT_WIDTH
SGU_HEADS = 8
SGU_HEAD_DIM = SGU_WIDTH // SGU_HEADS
CHUNK = 128
MIX_WIDTH = ATT_WIDTH + SGU_WIDTH
IN_COLS = ATT_WIDTH + 2 * KV_WIDTH + 2 * SGU_WIDTH
D_FF = 5504
CONV_W = 3
EPS = 1e-6
NEG_INF = -1e30

kernel_name = 'hymba_swa_sink_sgu_convffn_step'


def _rms(x, g):
    xf = x.astype(jnp.float32)
    y = xf * lax.rsqrt(jnp.mean(xf * xf, axis=-1, keepdims=True) + EPS)
    return (y * g.astype(jnp.float32)).astype(x.dtype)


def _layernorm(x, g, b):
    xf = x.astype(jnp.float32)
    mu = jnp.mean(xf, axis=-1, keepdims=True)
    xc = xf - mu
    y = xc * lax.rsqrt(jnp.mean(xc * xc, axis=-1, keepdims=True) + EPS)
    return (y * g.astype(jnp.float32) + b.astype(jnp.float32)).astype(x.dtype)


def _alibi_slopes():
    h = jnp.arange(1, ATT_HEADS + 1, dtype=jnp.float32)
    return jnp.exp2(-8.0 * h / ATT_HEADS).reshape(KV_HEADS, GROUP)


def _project(xn, w_in, g_q, g_k, ln_g, ln_b):
    z = xn @ w_in
    lead = z.shape[:-1]
    c1 = ATT_WIDTH
    c2 = c1 + KV_WIDTH
    c3 = c2 + KV_WIDTH
    c4 = c3 + SGU_WIDTH
    q = _rms(z[..., :c1].reshape(lead + (ATT_HEADS, HEAD_DIM)), g_q)
    k = _rms(z[..., c1:c2].reshape(lead + (KV_HEADS, HEAD_DIM)), g_k)
    v = z[..., c2:c3].reshape(lead + (KV_HEADS, HEAD_DIM))
    u = jax.nn.gelu(z[..., c3:c4])
    vs = _layernorm(jax.nn.gelu(z[..., c4:]), ln_g, ln_b).reshape(lead + (SGU_HEADS, SGU_HEAD_DIM))
    return q, k, v, u, vs


def _band_attention(q, k, v, q_pos, k_pos, sinks):
    B, N, Tq = q.shape[:3]
    qg = q.reshape(B, N, Tq, KV_HEADS, GROUP, HEAD_DIM)
    s = jnp.einsum('bnqkgd,bnskd->bnkgqs', qg, k,
                   preferred_element_type=jnp.float32) * (HEAD_DIM ** -0.5)
    dist = q_pos[:, :, None] - k_pos[:, None, :]
    valid = (dist >= 0) & (dist < WINDOW) & (k_pos[:, None, :] >= 0)
    slopes = _alibi_slopes()[None, None, :, :, None, None]
    s = s - slopes * dist.astype(jnp.float32)[None, :, None, None]
    s = jnp.where(valid[None, :, None, None], s, NEG_INF)
    sink = jnp.broadcast_to(
        sinks.astype(jnp.float32).reshape(KV_HEADS, GROUP)[None, None, :, :, None, None],
        s.shape[:-1] + (1,))
    p = jax.nn.softmax(jnp.concatenate([s, sink], axis=-1), axis=-1)[..., :-1]
    o = jnp.einsum('bnkgqs,bnskd->bnqkgd', p.astype(v.dtype), v)
    return o.reshape(B, N, Tq, ATT_WIDTH)


def _sgu(u, vs, w_s, b_s):
    T = u.shape[2]
    w = jnp.tril(w_s)[:, :T, :T]
    mixed = jnp.einsum('hts,bcshe->bcthe', w.astype(vs.dtype), vs) + b_s[:, :T].T[:, :, None]
    return u * mixed


def _merge(att, sgu, g_oa, g_os, w_o):
    return jnp.concatenate([_rms(att, g_oa), _rms(sgu, g_os)], axis=-1) @ w_o


def _conv_ffn(xn, buf, w_up, conv_w, conv_b, w_down):
    h = xn @ w_up
    T = h.shape[1]
    hp = jnp.concatenate([buf.astype(h.dtype), h], axis=1)
    hc = conv_b + conv_w[0] * hp[:, 0:T]
    for j in range(1, CONV_W):
        hc = hc + conv_w[j] * hp[:, j:j + T]
    a, g = jnp.split(hc, 2, axis=-1)
    return (jax.nn.silu(g) * a) @ w_down, hp[:, -(CONV_W - 1):]


def _prompt_layer(x, p):
    (g_mix, w_in, g_q, g_k, sinks, ln_g, ln_b, w_s, b_s,
     g_oa, g_os, w_o, g_ffn, w_up, conv_w, conv_b, w_down) = p
    B, S, _ = x.shape
    nb = S // BLOCK
    q, k, v, u, vs = _project(_rms(x, g_mix), w_in, g_q, g_k, ln_g, ln_b)
    qb = q.reshape(B, nb, BLOCK, ATT_HEADS, HEAD_DIM)
    kb = k.reshape(B, nb, BLOCK, KV_HEADS, HEAD_DIM)
    vb = v.reshape(B, nb, BLOCK, KV_HEADS, HEAD_DIM)

    def band(t):
        prev = jnp.pad(t, ((0, 0), (1, 0), (0, 0), (0, 0), (0, 0)))[:, :-1]
        return jnp.concatenate([prev, t], axis=2)

    pos = jnp.arange(S, dtype=jnp.int32).reshape(nb, BLOCK)
    k_pos = jnp.concatenate([pos - BLOCK, pos], axis=-1)
    att = _band_attention(qb, band(kb), band(vb), pos, k_pos, sinks).reshape(B, S, ATT_WIDTH)
    nc = S // CHUNK
    sgu = _sgu(u.reshape(B, nc, CHUNK, SGU_HEADS, SGU_HEAD_DIM),
               vs.reshape(B, nc, CHUNK, SGU_HEADS, SGU_HEAD_DIM), w_s, b_s).reshape(B, S, SGU_WIDTH)
    x = x + _merge(att, sgu, g_oa, g_os, w_o)
    zero_buf = jnp.zeros((B, CONV_W - 1, 2 * D_FF), x.dtype)
    y, conv_state = _conv_ffn(_rms(x, g_ffn), zero_buf, w_up, conv_w, conv_b, w_down)
    return x + y, k[:, -WINDOW:], v[:, -WINDOW:], conv_state


def _sample_layer(x, k_buf, v_buf, conv_buf, p):
    (g_mix, w_in, g_q, g_k, sinks, ln_g, ln_b, w_s, b_s,
     g_oa, g_os, w_o, g_ffn, w_up, conv_w, conv_b, w_down) = p
    Bd, T, _ = x.shape
    q, k, v, u, vs = _project(_rms(x, g_mix), w_in, g_q, g_k, ln_g, ln_b)
    kc = jnp.concatenate([k_buf.astype(k.dtype), k], axis=1)
    vc = jnp.concatenate([v_buf.astype(v.dtype), v], axis=1)
    q_pos = (PAST_LEN + jnp.arange(T, dtype=jnp.int32))[None]
    k_pos = (PAST_LEN - WINDOW + jnp.arange(WINDOW + T, dtype=jnp.int32))[None]
    att = _band_attention(q[:, None], kc[:, None], vc[:, None], q_pos, k_pos, sinks).reshape(Bd, T, ATT_WIDTH)
    sgu = _sgu(u.reshape(Bd, 1, T, SGU_HEADS, SGU_HEAD_DIM), vs[:, None], w_s, b_s).reshape(Bd, T, SGU_WIDTH)
    x = x + _merge(att, sgu, g_oa, g_os, w_o)
    y, conv_state = _conv_ffn(_rms(x, g_ffn), conv_buf, w_up, conv_w, conv_b, w_down)
    return x + y, kc[:, -WINDOW:], vc[:, -WINDOW:], conv_state, vs


def setup_inputs(seed: int = 0) -> dict:
    key = jax.random.key(seed)
    ks = jax.random.split(key, 24)
    f32 = jnp.float32
    nrm = lambda k, shape, scale: jax.random.normal(k, shape, f32) * scale
    gain = lambda k, shape: 1.0 + 0.02 * jax.random.normal(k, shape, f32)
    return {
        'x_prompt': nrm(ks[0], (BATCH, SEQ, D_MODEL), 1.0),
        'x_sample': nrm(ks[1], (DEC_BATCH, DEC_SEQ, D_MODEL), 1.0),
        'cache_k_win': nrm(ks[2], (DEPTH, DEC_BATCH, WINDOW, KV_HEADS, HEAD_DIM), 1.0),
        'cache_v_win': nrm(ks[3], (DEPTH, DEC_BATCH, WINDOW, KV_HEADS, HEAD_DIM), 1.0),
        'state_ffn_conv': nrm(ks[4], (DEPTH, DEC_BATCH, CONV_W - 1, 2 * D_FF), 1.0),
        'norm_mix_g': gain(ks[5], (DEPTH, D_MODEL)),
        'w_in': nrm(ks[6], (DEPTH, D_MODEL, IN_COLS), D_MODEL ** -0.5),
        'q_norm_g': gain(ks[7], (DEPTH, HEAD_DIM)),
        'k_norm_g': gain(ks[8], (DEPTH, HEAD_DIM)),
        'attn_sinks': nrm(ks[9], (DEPTH, ATT_HEADS), 0.5),
        'sgu_ln_g': gain(ks[10], (DEPTH, SGU_WIDTH)),
        'sgu_ln_b': nrm(ks[11], (DEPTH, SGU_WIDTH), 0.02),
        'sgu_w': nrm(ks[12], (DEPTH, SGU_HEADS, CHUNK, CHUNK), CHUNK ** -0.5),
        'sgu_b': gain(ks[13], (DEPTH, SGU_HEADS, CHUNK)),
        'out_norm_att_g': gain(ks[14], (DEPTH, ATT_WIDTH)),
        'out_norm_sgu_g': gain(ks[15], (DEPTH, SGU_WIDTH)),
        'w_o': nrm(ks[16], (DEPTH, MIX_WIDTH, D_MODEL), 0.5 * MIX_WIDTH ** -0.5),
        'norm_ffn_g': gain(ks[17], (DEPTH, D_MODEL)),
        'w_up': nrm(ks[18], (DEPTH, D_MODEL, 2 * D_FF), D_MODEL ** -0.5),
        'conv_w': nrm(ks[19], (DEPTH, CONV_W, 2 * D_FF), CONV_W ** -0.5),
        'conv_b': nrm(ks[20], (DEPTH, 2 * D_FF), 0.02),
        'w_down': nrm(ks[21], (DEPTH, D_FF, D_MODEL), 0.5 * D_FF ** -0.5),
    }


def reference(x_prompt, x_sample, cache_k_win, cache_v_win, state_ffn_conv,
              norm_mix_g, w_in, q_norm_g, k_norm_g, attn_sinks, sgu_ln_g, sgu_ln_b,
              sgu_w, sgu_b, out_norm_att_g, out_norm_sgu_g, w_o, norm_ffn_g,
              w_up, conv_w, conv_b, w_down):
    xp, xs = x_prompt, x_sample
    kp_l, vp_l, cp_l, ks_l, vs_l, cs_l, sv_l = [], [], [], [], [], [], []
    for l in range(DEPTH):
        p = (norm_mix_g[l], w_in[l], q_norm_g[l], k_norm_g[l], attn_sinks[l], sgu_ln_g[l],
             sgu_ln_b[l], sgu_w[l], sgu_b[l], out_norm_att_g[l], out_norm_sgu_g[l], w_o[l],
             norm_ffn_g[l], w_up[l], conv_w[l], conv_b[l], w_down[l])
        xp, kp, vp, cp = _prompt_layer(xp, p)
        xs, kn, vn, cn, svn = _sample_layer(xs, cache_k_win[l], cache_v_win[l], state_ffn_conv[l], p)
        kp_l.append(kp); vp_l.append(vp); cp_l.append(cp)
        ks_l.append(kn); vs_l.append(vn); cs_l.append(cn); sv_l.append(svn)
    k_win_prompt = jnp.stack(kp_l)
    v_win_prompt = jnp.stack(vp_l)
    ffn_conv_prompt = jnp.stack(cp_l)
    k_win_sample = jnp.stack(ks_l)
    v_win_sample = jnp.stack(vs_l)
    ffn_conv_sample = jnp.stack(cs_l)
    sgu_v_sample = jnp.stack(sv_l)
    return (xp, xs, k_win_prompt, v_win_prompt, ffn_conv_prompt,
            k_win_sample, v_win_sample, ffn_conv_sample, sgu_v_sample)
```

```python
import contextlib
import numpy as np
import concourse.bass as bass
import concourse.mybir as mybir
from concourse.bass_utils import run_bass_kernel_spmd

F32 = mybir.dt.float32
BF16 = mybir.dt.bfloat16
AF = mybir.ActivationFunctionType
ALU = mybir.AluOpType
AX = mybir.AxisListType

D = 2048
ND = 16
HD = 64
NH = 16
DFF = 5504
NFC = 43
S = 2048
TPT = 512
NTILES = 4
NBLK = 4
NSQ = 16
NST = 4
NSAMP = 64
TTMAX = TPT + NSAMP
NTOK = S + NSAMP
EPS = 1e-6
NSLOT = 5
SLOT_ELEMS = 4096
FG = [(0, 15), (15, 29), (29, 43)]
NCORES = 8
DEPTH = 2


def weight_layout():
    slots = []
    off = 0

    def add(kind, idx, ne):
        nonlocal off
        slots.append((kind, idx, off, ne))
        off += ne

    for s in range(4):
        add('vs', s, 16 * 256)
    for s in range(4):
        add('u', s, 16 * 256)
    for s in range(4):
        add('q', s, 16 * 256)
    add('kd', 0, 16 * 256)
    add('kv', 0, 16 * 256)
    for s in range(8):
        add('o', s, 16 * 256)
    for g, (f0, f1) in enumerate(FG):
        for j in range(f0, f1):
            add('up', j, 16 * 256)
        for m in range(16):
            add('dn', (g, m), (f1 - f0) * 128)
    return slots, off


SLOTS, EPL = weight_layout()


class DSem:
    def __init__(self, sh, sid):
        self.sh = sh
        self.sid = sid
        self.val = 0


class Em:
    def __init__(self, nc):
        self.nc = nc
        self.eng = {'pe': nc.tensor, 'act': nc.scalar, 'dve': nc.vector, 'pool': nc.gpsimd, 'sp': nc.sync}
        self.sem = {}
        self.cnt = {}
        self.seen = {e: {} for e in self.eng}
        self.lw = {}
        self.rd = {}
        self.nsem = 0
        self.floor = []
        self.ring = []
        self.ring_i = 0
        self.all_dsems = []
        self.new_epoch()

    def _new_sem(self, name):
        self.nsem += 1
        sh = self.nc.alloc_semaphore(f"{name}_{self.nsem}")
        return sh, self.nsem

    def new_dsem(self, name):
        sh, sid = self._new_sem(name)
        d = DSem(sh, sid)
        self.all_dsems.append(d)
        return d

    def new_epoch(self):
        for e in ('pe', 'act', 'dve', 'pool'):
            self.sem[e] = self._new_sem(e)
            self.cnt[e] = 0

    def _wait(self, eng, deps):
        need = {}
        for d in list(deps) + self.floor:
            if d is None:
                continue
            sid, sh, v, src = d
            if src == eng and eng == 'pe':
                continue
            if v <= self.seen[eng].get(sid, 0):
                continue
            if sid not in need or need[sid][1] < v:
                need[sid] = (sh, v)
        for sid, (sh, v) in need.items():
            self.eng[eng].wait_ge(sh, v)
            self.seen[eng][sid] = v

    def _deps(self, reads, writes):
        deps = []
        for k in reads:
            deps.append(self.lw.get(k))
        for k in writes:
            deps.append(self.lw.get(k))
            deps.extend(self.rd.get(k, {}).values())
        return deps

    def _record(self, tok, reads, writes):
        for k in writes:
            self.lw[k] = tok
            self.rd[k] = {}
        for k in reads:
            r = self.rd.setdefault(k, {})
            o = r.get(tok[0])
            if o is None or o[2] < tok[2]:
                r[tok[0]] = tok

    def op(self, eng, fn, reads=(), writes=()):
        self._wait(eng, self._deps(reads, writes))
        ins = fn(self.eng[eng])
        sh, sid = self.sem[eng]
        self.cnt[eng] += 1
        ins.then_inc(sh, 1)
        tok = (sid, sh, self.cnt[eng], eng)
        self._record(tok, reads, writes)
        return tok

    def pe_group(self, mms, reads=(), writes=()):
        self._wait('pe', self._deps(reads, writes))
        ins = None
        for (o, l, r, st, sp) in mms:
            ins = self.nc.tensor.matmul(o, lhsT=l, rhs=r, start=st, stop=sp)
        sh, sid = self.sem['pe']
        self.cnt['pe'] += 1
        ins.then_inc(sh, 1)
        tok = (sid, sh, self.cnt['pe'], 'pe')
        self._record(tok, reads, writes)
        return tok

    def dma(self, q, out, in_, reads=(), writes=(), ds=None):
        if ds is None:
            if not self.ring:
                self.ring = [self.new_dsem("dr") for _ in range(12)]
                self.ring_p = [self.new_dsem("dp") for _ in range(6)]
            rg = self.ring_p if q == 'pool' else self.ring
            ds = rg[self.ring_i % len(rg)]
            self.ring_i += 1
        deps = self._deps(reads, writes)
        if ds.val > 0:
            deps.append((ds.sid, ds.sh, ds.val, 'dma'))
        self._wait(q, deps)
        ins = self.eng[q].dma_start(out=out, in_=in_)
        ds.val += 16
        ins.then_inc(ds.sh, 16)
        tok = (ds.sid, ds.sh, ds.val, 'dma')
        self._record(tok, reads, writes)
        return tok

    def barrier(self, exclude=()):
        ex = {(tk[0], tk[2]) for tk in exclude}
        fl = []
        for e in ('pe', 'act', 'dve', 'pool'):
            sh, sid = self.sem[e]
            if self.cnt[e] > 0:
                fl.append((sid, sh, self.cnt[e], 'bar'))
        for d in self.all_dsems:
            if d.val > 0 and (d.sid, d.val) not in ex:
                fl.append((d.sid, d.sh, d.val, 'bar'))
        self.floor = fl

    def finish(self):
        self.barrier()
        self._wait('sp', [])


class StopBuild(Exception):
    pass


def build_program(stop=None):
    nc = bass.Bass("TRN2", target_bir_lowering=False)
    em = Em(nc)

    def ckpt(name):
        if stop is not None and stop == name:
            raise StopBuild(name)


    def din(name, shape, dt=F32):
        return nc.dram_tensor(name, list(shape), dt, kind="ExternalInput").ap()

    def dout(name, shape, dt=F32):
        return nc.dram_tensor(name, list(shape), dt, kind="ExternalOutput").ap()

    xT_d = din("xT", [128, ND, NTOK])
    wts_d = din("wts", [DEPTH, 128, EPL])
    ckT_d = din("ckT", [DEPTH, NSQ, 2, HD, 128])
    ck_d = din("ck", [DEPTH, NSQ, 128, 128])
    cv_d = din("cv", [DEPTH, NSQ, 128, 128])
    cs_d = din("cs", [DEPTH, 128, 2 * NFC, NSQ, 2])
    gvec_d = din("gvec", [128, DEPTH, 59])
    skrow_d = din("skrow", [1, DEPTH * 8 * 128])
    cwb_d = din("cwb", [128, DEPTH, 2 * NFC, 4])
    lngb_d = din("lngb", [DEPTH, 2, 128, 1024])
    gkb_d = din("gkb", [DEPTH, 128, 128])
    wsT_d = din("wsT", [DEPTH, 128, 8, 128])
    wsS_d = din("wsS", [DEPTH, NSAMP, 8, NSAMP])
    bsr_d = din("bsr", [DEPTH, 8, 128])
    bss_d = din("bss", [DEPTH, 8, NSAMP])
    maskp_d = din("maskp", [128, 4, 2, 2, 256])
    masks_d = din("masks", [NSAMP, NSQ, 2, 2, 4 * NST])
    masksp_d = din("masksp", [128, 2, 2, 4 * NST])
    tril_d = din("tril", [128, 128])
    trils_d = din("trils", [NSAMP, NSAMP])
    cmat_d = din("cmat", [128, 4, 128])

    yT_d = dout("yT", [128, ND, NTOK])
    kwp_d = dout("kwp", [DEPTH, 128, 128])
    vwp_d = dout("vwp", [DEPTH, 128, 128])
    cvp_d = dout("cvp", [DEPTH, 128, 2 * NFC, 2])
    kws_d = dout("kws", [DEPTH, NSQ, 128, 128])
    vws_d = dout("vws", [DEPTH, NSQ, 128, 128])
    cvs_d = dout("cvs", [DEPTH, 128, 2 * NFC, NSQ, 2])
    sgv_d = dout("sgv", [DEPTH, NSAMP, 1024])

    sb = nc.alloc_sbuf_tensor
    XT = sb("XT", [128, ND, TTMAX], F32)
    XN = sb("XN", [128, ND, TTMAX], BF16)
    WS = [sb(f"WS{i}", [128, SLOT_ELEMS], BF16) for i in range(NSLOT)]
    CMAT = sb("CMAT", [128, 4, 128], BF16)
    ONES = CMAT[:, 0, :]
    BD64 = CMAT[:, 1, :]
    MASKP = sb("MASKP", [128, 4, 2, 2, 256], BF16)
    MASKS = sb("MASKS", [NSAMP, NSQ, 2, 2, 4 * NST], BF16)
    MASKSP = sb("MASKSP", [128, 2, 2, 4 * NST], BF16)
    EPSC = sb("EPSC", [128, 1], F32)
    GV_ = sb("GVEC", [128, DEPTH, 59], F32)
    ESK = sb("ESK", [128, DEPTH, 8], F32)
    CWB = sb("CWB", [128, DEPTH, 2 * NFC, 4], F32)
    WST = sb("WST", [128, DEPTH, 8, 128], BF16)
    WSS = sb("WSS", [NSAMP, DEPTH, 8, NSAMP], BF16)
    BSR = sb("BSR", [128, DEPTH, 8, 128], BF16)
    KDC = sb("KDC", [128, DEPTH, 2, 128], BF16)
    VFB = sb("VFB", [128, NBLK, 4, 128], BF16)
    VFC = sb("VFC", [128, DEPTH, 4, 128], BF16)
    VFN = sb("VFN", [NSAMP, 4, 128], BF16)
    CONVC = sb("CONVC", [128, DEPTH, 2 * NFC, 2], F32)
    KDS = sb("KDS", [128, 2, 2, 128], BF16)
    VFS = sb("VFS", [128, 2, 4, 128], BF16)
    ZQ = sb("ZQ", [128, 2, TTMAX], F32)
    SQ = sb("SQ", [128, 2, TTMAX], BF16)
    RS = sb("RS", [128, 2, TTMAX], F32)
    VS = sb("VS", [128, NBLK + 1, 1024], BF16)

    PS = [nc.alloc_psum_tensor(f"ps{r}", [128, 1024], F32) for r in range(4)]
    ps_i = [0]

    ps_reserved = set()

    def next_ps():
        while True:
            r = ps_i[0] % 4
            ps_i[0] += 1
            if r not in ps_reserved:
                return PS[r], ("ps", r)

    def gcol(l, off, c):
        return GV_[:, l, off + c: off + c + 1]

    G_MIX, G_FFN, G_OA, G_OS, G_SK, G_Q, G_K = 0, 16, 32, 40, 48, 56, 57

    wsems = [em.new_dsem(f"ws{i}") for i in range(NSLOT)]
    wseq = [(l, si) for _t in range(NTILES) for l in range(DEPTH) for si in range(len(SLOTS))]
    wstate = {'issue': 0, 'use': 0, 'rel': 0}

    def w_issue():
        i = wstate['issue']
        if i >= len(wseq):
            return
        l, si = wseq[i]
        kind, idx, off, ne = SLOTS[si]
        b = i % NSLOT
        em.dma('pool', WS[b][:, 0:ne], wts_d[l, :, off:off + ne], writes=[("ws", b)], ds=wsems[b])
        wstate['issue'] += 1

    def w_acquire(kind, idx, l, hold=False):
        i = wstate['use']
        wstate['use'] += 1
        if not hold:
            wstate['rel'] = i
        while wstate['issue'] < min(len(wseq), wstate['rel'] + NSLOT):
            w_issue()
        ll, si = wseq[i]
        assert ll == l and SLOTS[si][0] == kind and SLOTS[si][1] == idx, (SLOTS[si], kind, idx, l, ll)
        assert i < wstate['issue']
        b = i % NSLOT
        return WS[b], ("ws", b)

    for _ in range(NSLOT):
        w_issue()

    cs_ = em.new_dsem("cst")
    with contextlib.ExitStack() as st:
        tmpf = st.enter_context(nc.sbuf_tensor("tmpf", [128, 4096], F32))
        tmpg = st.enter_context(nc.sbuf_tensor("tmpg", [128, 4096], F32))

        def ld(out, in_, keys):
            em.dma('sp', out, in_, writes=keys)

        ld(GV_[:], gvec_d, [("gvec",)])
        ld(CWB[:], cwb_d, [("cwb",)])
        ld(tmpf[:, 0:512], cmat_d.rearrange("p a b -> p (a b)"), [("tmpf",)])
        em.op('dve', lambda e: e.tensor_copy(out=CMAT[:].rearrange("p a b -> p (a b)"), in_=tmpf[:, 0:512]),
              reads=[("tmpf",)], writes=[("cmat",)])
        ld(tmpg[:, 0:4096], maskp_d.rearrange("p a b c q -> p (a b c q)"), [("tmpg",)])
        em.op('dve', lambda e: e.tensor_copy(out=MASKP[:].rearrange("p a b c q -> p (a b c q)"), in_=tmpg[:, 0:4096]),
              reads=[("tmpg",)], writes=[("maskp",)])
        ld(tmpf[0:NSAMP, 0:1024], masks_d.rearrange("p b a c q -> p (b a c q)"), [("tmpf",)])
        em.op('dve', lambda e: e.tensor_copy(out=MASKS[:].rearrange("p b a c q -> p (b a c q)"), in_=tmpf[0:NSAMP, 0:1024]),
              reads=[("tmpf",)], writes=[("masks",)])
        ld(tmpf[:, 1024:1088], masksp_d.rearrange("p g e q -> p (g e q)"), [("tmpf2",)])
        em.op('dve', lambda e: e.tensor_copy(out=MASKSP[:].rearrange("p g e q -> p (g e q)"), in_=tmpf[:, 1024:1088]),
              reads=[("tmpf2",)], writes=[("masks",)])
        TR = st.enter_context(nc.sbuf_tensor("TR", [128, 128], F32))
        TRS = st.enter_context(nc.sbuf_tensor("TRS", [NSAMP, NSAMP], F32))
        ld(TR[:], tril_d, [("tr",)])
        ld(TRS[:], trils_d, [("trs",)])
        for l in range(DEPTH):
            ld(tmpg[:, 0:1024], wsT_d[l].rearrange("p h t -> p (h t)"), [("tmpg",)])
            em.op('dve', lambda e, l=l: e.tensor_tensor(
                out=WST[:, l, :, :], in0=tmpg[:, 0:1024].rearrange("p (h t) -> p h t", h=8),
                in1=TR[:].rearrange("p (o t) -> p o t", o=1).broadcast_to([128, 8, 128]), op=ALU.mult),
                reads=[("tmpg",), ("tr",)], writes=[("wst", l)])
            ld(tmpf[0:NSAMP, 0:512], wsS_d[l].rearrange("p h t -> p (h t)"), [("tmpf",)])
            em.op('dve', lambda e, l=l: e.tensor_tensor(
                out=WSS[:, l, :, :], in0=tmpf[0:NSAMP, 0:512].rearrange("p (h t) -> p h t", h=8),
                in1=TRS[:].rearrange("p (o t) -> p o t", o=1).broadcast_to([NSAMP, 8, NSAMP]), op=ALU.mult),
                reads=[("tmpf",), ("trs",)], writes=[("wss", l)])
            ld(tmpg[0:1, 0:1024], bsr_d[l:l + 1].rearrange("o h t -> o (h t)"), [("tmpg",)])
            em.op('dve', lambda e, l=l: e.tensor_copy(out=BSR[0:1, l, :, :].rearrange("p h t -> p (h t)"),
                                                     in_=tmpg[0:1, 0:1024]),
                  reads=[("tmpg",)], writes=[("bsr", l)])
            ld(tmpf[32:33, 0:512], bss_d[l:l + 1].rearrange("o h t -> o (h t)"), [("tmpf",)])
            em.op('dve', lambda e, l=l: e.tensor_copy(out=BSR[32:33, l, :, 0:NSAMP],
                                                     in_=tmpf[32:33, 0:512].rearrange("p (h t) -> p h t", h=8)),
                  reads=[("tmpf",)], writes=[("bss", l)])
        tmpb = st.enter_context(nc.sbuf_tensor("tmpb", [128, 2048], BF16))
        ld(tmpf[64:65, 0:2048], skrow_d, [("tmpf",)])
        ld(tmpf[65:66, 0:2048], skrow_d, [("tmpf",)])
        A_ = tmpf[64:66, 0:2048]
        B_ = tmpg[64:66, 0:2048]
        C_ = tmpb[64:66, 0:2048]
        em.op('act', lambda e: e.activation(out=B_, in_=A_, func=AF.Exp), reads=[("tmpf",)], writes=[("tmpg",)])
        em.op('dve', lambda e: e.tensor_copy(out=C_, in_=B_), reads=[("tmpg",)], writes=[("tmpb",)])
        em.op('dve', lambda e: e.tensor_copy(out=A_, in_=C_), reads=[("tmpb",)], writes=[("tmpf",)])
        em.op('dve', lambda e: e.tensor_tensor(out=B_, in0=B_, in1=A_, op=ALU.subtract),
              reads=[("tmpf",), ("tmpg",)], writes=[("tmpg",)])
        em.op('dve', lambda e: e.tensor_tensor(out=A_, in0=A_, in1=B_, op=ALU.subtract),
              reads=[("tmpf",), ("tmpg",)], writes=[("tmpf",)])
        em.op('dve', lambda e: e.scalar_tensor_tensor(out=BSR[64:66, :, :, :].rearrange("p l m q -> p (l m q)"),
                                                      in0=A_, scalar=GV_[64:66, 0, 58:59], in1=B_,
                                                      op0=ALU.mult, op1=ALU.add),
              reads=[("tmpf",), ("tmpg",), ("gvec",)], writes=[("esr",)])
        em.dma('sp', XT[:, :, 0:TPT], xT_d[:, :, 0:TPT], writes=[("xt", c) for c in range(ND)])
        em.dma('sp', XT[:, :, TPT:TTMAX], xT_d[:, :, S:S + NSAMP], writes=[("xts",)])
        em.op('dve', lambda e: e.memset(EPSC[:], EPS), writes=[("eps",)])
        em.op('dve', lambda e: e.memset(VFB[:].rearrange("p a c d -> p (a c d)"), 0.0), writes=[("vfz",)])
        em.op('dve', lambda e: e.memset(VFC[:].rearrange("p a c d -> p (a c d)"), 0.0), writes=[("vfcz",)])
        em.op('dve', lambda e: e.memset(VFN[:].rearrange("p c d -> p (c d)"), 0.0), writes=[("vfnz",)])
        em.op('dve', lambda e: e.memset(VFS[:].rearrange("p a c d -> p (a c d)"), 0.0), writes=[("vfsz",)])
        em.op('dve', lambda e: e.memset(CONVC[:].rearrange("p a b c -> p (a b c)"), 0.0), writes=[("convc",)])
        em.op('act', lambda e: e.activation(out=ESK[:, :, :],
                                            in_=GV_[:, :, G_SK:G_SK + 8], func=AF.Exp),
              reads=[("gvec",)], writes=[("esk",)])
        em.barrier()

    def rstd_from_ps(ps, pk, ncols, scale, ring_i, parts=128):
        r1 = RS[0:parts, ring_i, 0:ncols]
        em.op('act', lambda e: e.activation(out=r1, in_=ps, func=AF.Ln, scale=scale, bias=EPSC[0:parts, 0:1]),
              reads=[pk, ("eps",)], writes=[("rs", ring_i)])
        em.op('act', lambda e: e.activation(out=r1, in_=r1, func=AF.Exp, scale=-0.5),
              reads=[("rs", ring_i)], writes=[("rs", ring_i)])
        return r1

    uid = [0]

    def sbt(name, shape, dt):
        uid[0] += 1
        return nc.sbuf_tensor(f"{name}_{uid[0]}", shape, dt)

    stats = {}

    def run_tile(t):
        TT = TTMAX if t == NTILES - 1 else TPT
        has_s = (t == NTILES - 1)
        nts = [(0, 512)] + ([(512, NSAMP)] if has_s else [])
        xkeys = [("xt", c) for c in range(ND)]
        xnkeys = [("xn", c) for c in range(ND)]

        def stats_begin(TTs=None, ntss=None):
            ps, pk = next_ps()
            ps_reserved.add(pk[1])
            stats['ps'], stats['pk'], stats['q'] = ps, pk, []
            stats['TT'], stats['nts'] = (TT if TTs is None else TTs), (nts if ntss is None else ntss)

        def stats_chunk(c):
            ps, pk = stats['ps'], stats['pk']
            TTs, ntss = stats['TT'], stats['nts']
            sq = SQ[:, c % 2, 0:TTs]
            em.op('act', lambda e, c=c, sq=sq: e.activation(out=sq, in_=XT[:, c, 0:TTs], func=AF.Square),
                  reads=[("xt", c)], writes=[("sq", c % 2)])
            em.pe_group([(ps[:, n0:n0 + nn], ONES, SQ[:, c % 2, n0:n0 + nn], c == 0, c == ND - 1)
                         for (n0, nn) in ntss], reads=[("sq", c % 2), ("cmat",)], writes=[pk])

        def stats_push(c, delay=2):
            stats['q'].append(c)
            while len(stats['q']) > delay:
                stats_chunk(stats['q'].pop(0))

        def stats_finish(l, goff):
            while stats['q']:
                stats_chunk(stats['q'].pop(0))
            ps, pk = stats['ps'], stats['pk']
            assert stats['TT'] == TT
            r = rstd_from_ps(ps[:, 0:TT], pk, TT, 1.0 / D, 0)
            ps_reserved.discard(pk[1])
            for c in range(ND):
                em.op('dve', lambda e, c=c: e.scalar_tensor_tensor(
                    out=XN[:, c, 0:TT], in0=XT[:, c, 0:TT], scalar=gcol(l, goff, c), in1=r,
                    op0=ALU.mult, op1=ALU.mult),
                    reads=[("xt", c), ("rs", 0), ("gvec",)], writes=[("xn", c)])

        def rms_to_xn(l, goff):
            stats_begin()
            for c in range(ND):
                stats_chunk(c)
            stats_finish(l, goff)

        def proj_fm(w, wk, col0, ps, pk, kc=ND, rhs=None, rkeys=None, stride=256, split_last=False):
            rkeys = xnkeys if rkeys is None else rkeys
            segs = [(0, kc - 1), (kc - 1, kc)] if split_last else [(0, kc)]
            for (k0, k1) in segs:
                mms = []
                for k in range(k0, k1):
                    for (n0, nn) in nts:
                        mms.append((ps[:, n0:n0 + nn], w[:, k * stride + col0:k * stride + col0 + 128],
                                    (XN if rhs is None else rhs)[:, k, n0:n0 + nn], k == 0, k == kc - 1))
                em.pe_group(mms, reads=[wk] + rkeys[k0:k1], writes=[pk])

        hn_i = [0]

        def headnorm(ps, pk, out_ap, okey, gain):
            i = hn_i[0] % 2
            hn_i[0] += 1
            em.op('act', lambda e: e.activation(out=ZQ[:, i, 0:TT], in_=ps[:, 0:TT], func=AF.Copy),
                  reads=[pk], writes=[("zq", i)])
            em.op('act', lambda e: e.activation(out=SQ[:, i, 0:TT], in_=ps[:, 0:TT], func=AF.Square),
                  reads=[pk], writes=[("sq", i)])
            yield
            ps2, pk2 = next_ps()
            em.pe_group([(ps2[:, n0:n0 + nn], BD64, SQ[:, i, n0:n0 + nn], True, True) for (n0, nn) in nts],
                        reads=[("sq", i), ("cmat",)], writes=[pk2])
            r = rstd_from_ps(ps2[:, 0:TT], pk2, TT, 1.0 / HD, i)
            em.op('dve', lambda e: e.scalar_tensor_tensor(out=out_ap, in0=ZQ[:, i, 0:TT], scalar=gain, in1=r,
                                                          op0=ALU.mult, op1=ALU.mult),
                  reads=[("zq", i), ("rs", i), ("gvec",)], writes=[okey])

        hn_pend = [None]

        def hn_push(gen):
            next(gen)
            hn_flush()
            hn_pend[0] = gen

        def hn_flush():
            if hn_pend[0] is not None:
                for _ in hn_pend[0]:
                    pass
                hn_pend[0] = None

        blocks = [(b * 128, 128, False) for b in range(NBLK)] + ([(TPT, NSAMP, True)] if has_s else [])

        for l in range(DEPTH):
            em.new_epoch()
            ckpt(f"start_{t}_{l}")
            if l == 0 and t == 0:
                rms_to_xn(l, G_MIX)
            else:
                stats_finish(l, G_MIX)
            ckpt(f"rms_{t}_{l}")
            with contextlib.ExitStack() as st:
                QT = st.enter_context(sbt("QT", [128, 8, TTMAX], BF16))
                KD = st.enter_context(sbt("KD", [128, 2, TTMAX], BF16))
                UT = st.enter_context(sbt("UT", [128, 8, TTMAX], BF16))
                KTF = st.enter_context(sbt("KTF", [128, 2, 128], F32))
                KTT = st.enter_context(sbt("KTT", [128, 128], F32))
                KST = st.enter_context(sbt("KST", [128, 8], F32))
                GKB = st.enter_context(sbt("GKB", [128, 128], F32))
                em.dma('sp', GKB[:], gkb_d[l], writes=[("gkb",)])

                st1 = contextlib.ExitStack()
                GVB = st1.enter_context(sbt("GVB", [128, 2, 1024], F32))
                LNG = st1.enter_context(sbt("LNG", [128, 2, 1024], F32))
                TMP = st1.enter_context(sbt("TMPA", [128, 1024], F32))
                STT = st1.enter_context(sbt("STT", [128, NBLK + 1, 12], F32))
                em.dma('sp', LNG[:, 0, :], lngb_d[l, 0], writes=[("lng",)])
                em.dma('sp', LNG[:, 1, :], lngb_d[l, 1], writes=[("lnb",)])
                wv = [w_acquire('vs', s, l, hold=(s > 0)) for s in range(4)]
                for bi, (c0, nt, smp) in enumerate(blocks):
                    gb_ = bi % 2
                    gk = [("gv", gb_, s) for s in range(4)]
                    for s in range(4):
                        w, wk = wv[s]
                        ps, pk = next_ps()
                        em.pe_group([(ps[0:nt, 0:256], XN[:, k, c0:c0 + nt], w[:, k * 256:(k + 1) * 256],
                                      k == 0, k == ND - 1) for k in range(ND)],
                                    reads=[wk] + xnkeys, writes=[pk])
                        em.op('act', lambda e, gb_=gb_, nt=nt, s=s, ps=ps, bi=bi: e.activation(
                            out=GVB[0:nt, gb_, s * 256:(s + 1) * 256], in_=ps[0:nt, 0:256], func=AF.Gelu_apprx_tanh,
                            accum_out=STT[0:nt, bi, s:s + 1]),
                            reads=[pk], writes=[("gv", gb_, s), ("st", bi, s)])
                    g = GVB[0:nt, gb_, :]
                    stt = STT[0:nt, bi, :]
                    sk_ = [("st", bi, s) for s in range(4)]
                    em.op('act', lambda e, g=g, stt=stt, nt=nt: e.activation(out=TMP[0:nt, :], in_=g, func=AF.Square,
                                                                            accum_out=stt[:, 4:5]),
                          reads=gk, writes=[("tmpa",), ("st", bi)])
                    em.op('dve', lambda e, stt=stt: e.reduce_sum(out=stt[:, 5:6], in_=stt[:, 0:4], axis=AX.X),
                          reads=sk_, writes=[("st5", bi)])
                    em.op('dve', lambda e, stt=stt: e.tensor_scalar(out=stt[:, 6:7], in0=stt[:, 5:6],
                                                                   scalar1=1.0 / 1024, scalar2=None, op0=ALU.mult),
                          reads=[("st5", bi)], writes=[("st6", bi)])
                    em.op('dve', lambda e, stt=stt: e.tensor_tensor(out=stt[:, 7:8], in0=stt[:, 6:7], in1=stt[:, 6:7],
                                                                   op=ALU.mult),
                          reads=[("st6", bi)], writes=[("st7", bi)])
                    em.op('dve', lambda e, stt=stt: e.scalar_tensor_tensor(out=stt[:, 8:9], in0=stt[:, 4:5],
                                                                          scalar=1.0 / 1024, in1=stt[:, 7:8],
                                                                          op0=ALU.mult, op1=ALU.subtract),
                          reads=[("st", bi), ("st7", bi)], writes=[("st8", bi)])
                    em.op('act', lambda e, stt=stt, nt=nt: e.activation(out=stt[:, 9:10], in_=stt[:, 8:9], func=AF.Ln,
                                                                       bias=EPSC[0:nt, 0:1]),
                          reads=[("st8", bi), ("eps",)], writes=[("st9", bi)])
                    em.op('act', lambda e, stt=stt: e.activation(out=stt[:, 10:11], in_=stt[:, 9:10], func=AF.Exp,
                                                                scale=-0.5),
                          reads=[("st9", bi)], writes=[("st10", bi)])
                    em.op('dve', lambda e, g=g, stt=stt: e.tensor_scalar(out=g, in0=g, scalar1=stt[:, 6:7],
                                                                        scalar2=stt[:, 10:11], op0=ALU.subtract,
                                                                        op1=ALU.mult),
                          reads=gk + [("st6", bi), ("st10", bi)], writes=gk)
                    em.op('dve', lambda e, g=g, nt=nt: e.tensor_tensor(out=TMP[0:nt, :], in0=g, in1=LNG[0:nt, 0, :],
                                                                      op=ALU.mult),
                          reads=gk + [("lng",)], writes=[("tmpa",)])
                    if smp:
                        em.op('dve', lambda e, g=g, nt=nt: e.tensor_tensor(out=g, in0=TMP[0:nt, :], in1=LNG[0:nt, 1, :],
                                                                          op=ALU.add),
                              reads=[("tmpa",), ("lnb",)], writes=gk)
                        em.op('act', lambda e, g=g, nt=nt, bi=bi: e.activation(out=VS[0:nt, bi, :], in_=g, func=AF.Copy),
                              reads=gk, writes=[("vs", bi)])
                        em.dma('sp', sgv_d[l], g, reads=gk)
                    else:
                        em.op('dve', lambda e, nt=nt, bi=bi: e.tensor_tensor(out=VS[0:nt, bi, :], in0=TMP[0:nt, :],
                                                                            in1=LNG[0:nt, 1, :], op=ALU.add),
                              reads=[("tmpa",), ("lnb",)], writes=[("vs", bi)])
                ckpt(f"vs_{t}_{l}")

                for s in range(4):
                    w, wk = w_acquire('u', s, l)
                    for mm in range(2):
                        h = 2 * s + mm
                        ps, pk = next_ps()
                        proj_fm(w, wk, mm * 128, ps, pk)
                        em.op('act', lambda e, h=h, ps=ps: e.activation(out=UT[:, h, 0:TT], in_=ps[:, 0:TT],
                                                                       func=AF.Gelu_apprx_tanh),
                              reads=[pk], writes=[("ut", h)])
                for s in range(4):
                    w, wk = w_acquire('q', s, l)
                    for mm in range(2):
                        m = 2 * s + mm
                        ps, pk = next_ps()
                        proj_fm(w, wk, mm * 128, ps, pk)
                        hn_push(headnorm(ps, pk, QT[:, m, 0:TT], ("qt", m), GV_[:, l, G_Q:G_Q + 1]))
                w, wk = w_acquire('kd', 0, l)
                for g in range(2):
                    ps, pk = next_ps()
                    proj_fm(w, wk, g * 128, ps, pk)
                    hn_push(headnorm(ps, pk, KD[:, g, 0:TT], ("kd", g), GV_[:, l, G_K:G_K + 1]))
                hn_flush()
                out_toks = []
                w, wk = w_acquire('kv', 0, l)
                for bi, (c0, nt, smp) in enumerate(blocks):
                    gb = t * NBLK + bi
                    ps, pk = next_ps()
                    em.pe_group([(ps[0:nt, 0:256], XN[:, k, c0:c0 + nt], w[:, k * 256:(k + 1) * 256],
                                  k == 0, k == ND - 1) for k in range(ND)], reads=[wk] + xnkeys, writes=[pk])
                    vsrc = ps[0:nt, 128:256].rearrange("p (g d) -> p g d", g=2)
                    if smp:
                        vdst = VFN[0:nt].rearrange("p (g e) d -> p g e d", g=2)
                        vkey = ("vfn",)
                    else:
                        vdst = VFB[0:nt, bi].rearrange("p (g e) d -> p g e d", g=2)
                        vkey = ("vf", bi)
                    em.op('act', lambda e, vdst=vdst, vsrc=vsrc: e.activation(out=vdst[:, :, 0, 0:64], in_=vsrc, func=AF.Copy),
                          reads=[pk], writes=[vkey])
                    em.op('act', lambda e, vdst=vdst, vsrc=vsrc: e.activation(out=vdst[:, :, 1, 64:128], in_=vsrc, func=AF.Copy),
                          reads=[pk], writes=[vkey])
                    last_prompt = (t == NTILES - 1 and bi == NBLK - 1)
                    if smp or last_prompt:
                        em.op('act', lambda e, nt=nt, ps=ps: e.activation(
                            out=KTF[0:nt].rearrange("p a b -> p (a b)"), in_=ps[0:nt, 0:256], func=AF.Copy),
                            reads=[pk], writes=[("ktf",)])
                        em.op('dve', lambda e, nt=nt: e.tensor_tensor(out=KTT[0:nt, :], in0=KTF[0:nt, 0, :],
                                                                     in1=KTF[0:nt, 0, :], op=ALU.mult),
                              reads=[("ktf",)], writes=[("ktt",)])
                        em.op('dve', lambda e, nt=nt: e.reduce_sum(out=KST[0:nt, 0:2],
                                                                  in_=KTT[0:nt, :].rearrange("p (g d) -> p g d", g=2),
                                                                  axis=AX.X),
                              reads=[("ktt",)], writes=[("kst",)])
                        em.op('act', lambda e, nt=nt: e.activation(out=KST[0:nt, 2:4], in_=KST[0:nt, 0:2], func=AF.Ln,
                                                                  scale=1.0 / HD, bias=EPSC[0:nt, 0:1]),
                              reads=[("kst",), ("eps",)], writes=[("kst",)])
                        em.op('act', lambda e, nt=nt: e.activation(out=KST[0:nt, 4:6], in_=KST[0:nt, 2:4], func=AF.Exp,
                                                                  scale=-0.5),
                              reads=[("kst",)], writes=[("kst",)])
                        for g in range(2):
                            em.op('dve', lambda e, nt=nt, g=g: e.scalar_tensor_tensor(
                                out=KTT[0:nt, g * 64:(g + 1) * 64], in0=KTF[0:nt, 0, g * 64:(g + 1) * 64],
                                scalar=KST[0:nt, 4 + g:5 + g], in1=GKB[0:nt, g * 64:(g + 1) * 64],
                                op0=ALU.mult, op1=ALU.mult),
                                reads=[("ktf",), ("kst",), ("gkb",)], writes=[("ktt",)])
                        if last_prompt:
                            out_toks.append(em.dma('sp', kwp_d[l], KTT[:, :], reads=[("ktt",)]))
                            out_toks.append(em.dma('sp', vwp_d[l], KTF[:, 1, :], reads=[("ktf",)]))
                        else:
                            for b in range(NSQ):
                                out_toks.append(em.dma('sp', kws_d[l, b, 124:128, :], KTT[4 * b:4 * b + 4, :], reads=[("ktt",)]))
                                out_toks.append(em.dma('sp', vws_d[l, b, 124:128, :], KTF[4 * b:4 * b + 4, 1, :], reads=[("ktf",)]))
                            out_toks.append(em.dma('sp', kws_d[l, :, 0:124, :], ck_d[l, :, 4:128, :]))
                            out_toks.append(em.dma('sp', vws_d[l, :, 0:124, :], cv_d[l, :, 4:128, :]))

                em.barrier(exclude=out_toks)
                st1.close()
                st2 = contextlib.ExitStack()
                EB = st2.enter_context(sbt("EB", [128, 1024], F32))
                PT = st2.enter_context(sbt("PT", [128, 3, 1024], BF16))
                if has_s:
                    KSTG = st2.enter_context(sbt("KSTG", [128, 2, 2, 128], F32))
                    VSTG = st2.enter_context(sbt("VSTG", [128, 2, 128], F32))
                    EBS = st2.enter_context(sbt("EBS", [128, 4, 64], F32))
                    PTS = st2.enter_context(sbt("PTS", [128, 4, 64], BF16))
                    DRS = st2.enter_context(sbt("DRS", [128, 4, 16], F32))
                ATT = st2.enter_context(sbt("ATT", [128, 8, 128], F32))
                SGO = st2.enter_context(sbt("SGO", [128, 8, 128], F32))
                SQB = st2.enter_context(sbt("SQB", [128, 16, 128], BF16))
                DR = st2.enter_context(sbt("DR", [128, 256], F32))
                ckpt(f"proj_{t}_{l}")
                pt_i = [0, 0]

                def attn_unit(m0, nm, g, c0, nq, prev, cur, mprev, mcur, att_out, small=False):
                    pss, pks = next_ps()
                    F = nm * nq
                    W = 2 * F
                    S4 = pss[:, 0:1024].rearrange("p (e x) -> p e x", e=2)[:, :, 0:W].rearrange(
                        "p e (c x) -> p e c x", c=2)
                    mms = []
                    rk = [("qt", m0 + i) for i in range(nm)]
                    for e_ in range(2):
                        q_ap = QT[64 * e_:64 * e_ + 64, m0:m0 + nm, c0:c0 + nq]
                        if prev is not None:
                            mms.append((S4[0:prev[2], e_, 0, :], prev[0](g, e_), q_ap, True, True))
                        mms.append((S4[0:cur[2], e_, 1, :], cur[0](g, e_), q_ap, True, True))
                    if prev is not None:
                        rk = rk + prev[3]
                    rk = rk + cur[3]
                    em.pe_group(mms, reads=rk, writes=[pks])
                    if small:
                        pi = pt_i[1] % 4
                        pt_i[1] += 1
                        ebuf, pbuf, dbuf = EBS[:, pi, :], PTS[:, pi, :], DRS[:, pi, :]
                        ekey, pkey, dkey, meng = ("ebs", pi), ("pts", pi), ("drs", pi), 'dve'
                    else:
                        pi = pt_i[0] % 3
                        pt_i[0] += 1
                        ebuf, pbuf, dbuf = EB[:, :], PT[:, pi, :], DR[:, :]
                        ekey, pkey, dkey, meng = ("eb",), ("pt", pi), ("dr",), 'pool'
                    E4 = ebuf[:, 0:2 * W].rearrange("p (e c x) -> p e c x", e=2, c=2)
                    P4 = pbuf[:, 0:2 * W].rearrange("p (e c x) -> p e c x", e=2, c=2)
                    parts = ([(0, prev[2], mprev)] if prev is not None else []) + [(1, cur[2], mcur)]
                    for (ch, nk, mk) in parts:
                        em.op('act', lambda e, ch=ch, nk=nk: e.activation(out=E4[0:nk, :, ch, :], in_=S4[0:nk, :, ch, :],
                                                                         func=AF.Exp, scale=0.125),
                              reads=[pks], writes=[ekey + (ch,)])
                        em.op(meng if ch == 0 else 'dve', lambda e, ch=ch, nk=nk, mk=mk: e.tensor_tensor(
                            out=P4[0:nk, :, ch, :], in0=E4[0:nk, :, ch, :], in1=mk, op=ALU.mult),
                            reads=[ekey + (ch,), ("maskp",), ("masks",)], writes=[pkey + (ch,)])
                    yield
                    pso, pko = next_ps()
                    O3 = pso[:, 0:W].rearrange("p (o x) -> p o x", o=2)
                    mms = []
                    for od in range(2):
                        lst = []
                        for e_ in range(2):
                            for (ch, nk, mk) in parts:
                                src = prev if ch == 0 else cur
                                lhs = src[1](g, e_) if od == 0 else CMAT[0:nk, 2 + e_, :]
                                lst.append((lhs, P4[0:nk, e_, ch, :]))
                        for i, (lhs, rhs) in enumerate(lst):
                            mms.append((O3[:, od, :], lhs, rhs, i == 0, (od == 0 and i == len(lst) - 1)))
                        if od == 1:
                            for mi in range(nm):
                                mms.append((O3[:, 1, mi * nq:(mi + 1) * nq], BSR[64:66, l, m0 + mi, :],
                                            CMAT[64:66, 0, 0:nq], False, mi == nm - 1))
                    rk = [pkey + (1,), ("cmat",), ("esr",)] + cur[4]
                    if prev is not None:
                        rk += [pkey + (0,)] + prev[4]
                    em.pe_group(mms, reads=rk, writes=[pko])
                    DRf = dbuf[:, 0:F]
                    em.op('act', lambda e: e.activation(out=DRf, in_=O3[:, 1, :], func=AF.Ln),
                          reads=[pko], writes=[dkey])
                    em.op('act', lambda e: e.activation(out=DRf, in_=DRf, func=AF.Exp, scale=-1.0),
                          reads=[dkey], writes=[dkey])
                    em.op('dve', lambda e: e.tensor_tensor(out=att_out,
                                                           in0=O3[:, 0, :].rearrange("p (m q) -> p m q", m=nm),
                                                           in1=DRf.rearrange("p (m q) -> p m q", m=nm), op=ALU.mult),
                          reads=[pko, dkey], writes=[("att", m0 + i) for i in range(nm)])

                def sgu_part(c0, nq, bi, smp):
                    psg, pkg = next_ps()
                    G3 = psg[:, 0:8 * nq].rearrange("p (h q) -> p h q", h=8)
                    mms = []
                    for h in range(8):
                        if smp:
                            wmat = WSS[0:nq, l, h, 0:nq]
                            brow = BSR[32:33, l, h, 0:nq]
                            orow = CMAT[32:33, 0, :]
                        else:
                            wmat = WST[0:nq, l, h, 0:nq]
                            brow = BSR[0:1, l, h, 0:nq]
                            orow = CMAT[0:1, 0, :]
                        mms.append((G3[:, h, :], VS[0:nq, bi, h * 128:(h + 1) * 128], wmat, True, False))
                        mms.append((G3[:, h, :], orow, brow, False, True))
                    em.pe_group(mms, reads=[("vs", bi), ("wst", l), ("wss", l), ("bsr", l), ("bss", l), ("cmat",)],
                                writes=[pkg])
                    em.op('dve', lambda e: e.tensor_tensor(out=SGO[:, :, 0:nq], in0=UT[:, :, c0:c0 + nq], in1=G3,
                                                           op=ALU.mult),
                          reads=[pkg] + [("ut", h) for h in range(8)], writes=[("sgo",)])

                def merge_part(c0, nq):
                    em.op('act', lambda e: e.activation(out=SQB[:, 0:8, 0:nq], in_=ATT[:, :, 0:nq], func=AF.Square),
                          reads=[("att", m) for m in range(8)], writes=[("sqb", 0)])
                    em.op('act', lambda e: e.activation(out=SQB[:, 8:16, 0:nq], in_=SGO[:, :, 0:nq], func=AF.Square),
                          reads=[("sgo",)], writes=[("sqb", 1)])
                    psn, pkn = next_ps()
                    N2 = psn[:, 0:2 * nq].rearrange("p (a q) -> p a q", a=2)
                    mms = []
                    for a in range(2):
                        for m in range(8):
                            mms.append((N2[:, a, :], ONES, SQB[:, 8 * a + m, 0:nq], m == 0, m == 7))
                    em.pe_group(mms, reads=[("sqb", 0), ("sqb", 1), ("cmat",)], writes=[pkn])
                    r = rstd_from_ps(psn[:, 0:2 * nq], pkn, 2 * nq, 1.0 / 1024, 1)
                    R2 = r.rearrange("p (a q) -> p a q", a=2)
                    Gb = lambda off: GV_[:, l, off:off + 8].rearrange("p (m o) -> p m o", o=1).broadcast_to([128, 8, nq])
                    em.op('dve', lambda e: e.tensor_tensor(out=ATT[:, :, 0:nq], in0=ATT[:, :, 0:nq], in1=Gb(G_OA), op=ALU.mult),
                          reads=[("att", m) for m in range(8)] + [("gvec",), ("sqb", 0)],
                          writes=[("att", m) for m in range(8)])
                    em.op('dve', lambda e: e.tensor_tensor(out=SGO[:, :, 0:nq], in0=SGO[:, :, 0:nq], in1=Gb(G_OS), op=ALU.mult),
                          reads=[("sgo",), ("gvec",), ("sqb", 1)], writes=[("sgo",)])
                    Rb = lambda a: r[:, a * nq:(a + 1) * nq].rearrange("p (o q) -> p o q", o=1).broadcast_to([128, 8, nq])
                    em.op('dve', lambda e: e.tensor_tensor(out=XN[:, 0:8, c0:c0 + nq], in0=ATT[:, :, 0:nq], in1=Rb(0),
                                                           op=ALU.mult),
                          reads=[("att", m) for m in range(8)] + [("rs", 1)], writes=[("xn", m) for m in range(8)])
                    em.op('dve', lambda e: e.tensor_tensor(out=XN[:, 8:16, c0:c0 + nq], in0=SGO[:, :, 0:nq], in1=Rb(1),
                                                           op=ALU.mult),
                          reads=[("sgo",), ("rs", 1)], writes=[("xn", 8 + m) for m in range(8)])

                def with_prelude(pre, gen):
                    pre()
                    yield from gen

                items = []
                for bi, (c0, nt, smp) in enumerate(blocks):
                    if not smp:
                        gb = t * NBLK + bi
                        if gb == 0:
                            prev = None
                        else:
                            if bi == 0:
                                kp = lambda g, e_: KDC[64 * e_:64 * e_ + 64, l, g, :]
                                kpk = [("kdc", l)]
                                vp = lambda g, e_: VFC[:, l, 2 * g + e_, :]
                                vpk = [("vfc", l), ("vfcz",)]
                            else:
                                kp = lambda g, e_, c0=c0: KD[64 * e_:64 * e_ + 64, g, c0 - 128:c0]
                                kpk = [("kd", 0), ("kd", 1)]
                                vp = lambda g, e_, bi=bi: VFB[:, bi - 1, 2 * g + e_, :]
                                vpk = [("vf", bi - 1), ("vfz",)]
                            prev = (kp, vp, 128, kpk, vpk)
                        kc_ = lambda g, e_, c0=c0: KD[64 * e_:64 * e_ + 64, g, c0:c0 + 128]
                        vc = lambda g, e_, bi=bi: VFB[:, bi, 2 * g + e_, :]
                        cur = (kc_, vc, 128, [("kd", 0), ("kd", 1)], [("vf", bi), ("vfz",)])
                        for mp in range(4):
                            gen = attn_unit(2 * mp, 2, mp // 2, c0, 128, prev, cur,
                                            MASKP[:, mp, :, 0, :], MASKP[:, mp, :, 1, :],
                                            ATT[:, 2 * mp:2 * mp + 2, :])
                            post = None
                            if mp == 3:
                                nxt = blocks[bi + 1] if bi + 1 < len(blocks) else None

                                def post(c0=c0, bi=bi, nxt=nxt):
                                    merge_part(c0, 128)
                                    if nxt is not None:
                                        sgu_part(nxt[0], nxt[1], bi + 1, nxt[2])
                            items.append((gen, post, 2))
                    else:
                        for b in range(NSQ):
                            ri = b % 2

                            def pre(b=b, ri=ri, c0=c0, bi=bi):
                                for e_ in range(2):
                                    em.dma('sp', KSTG[64 * e_:64 * e_ + 64, ri, :, :],
                                           ckT_d[l, b].rearrange("g d k -> d g k"), writes=[("kstg", ri)])
                                em.dma('sp', VSTG[:, ri, :], cv_d[l, b], writes=[("vstg", ri)])
                                em.op('dve', lambda e: e.tensor_copy(out=KDS[:, ri, :, :], in_=KSTG[:, ri, :, :]),
                                      reads=[("kstg", ri)], writes=[("kds", ri)])
                                vd = VFS[:, ri].rearrange("p (g e) d -> p g e d", g=2)
                                vsrc = VSTG[:, ri, :].rearrange("p (g d) -> p g d", g=2)
                                em.op('dve', lambda e: e.tensor_copy(out=vd[:, :, 0, 0:64], in_=vsrc),
                                      reads=[("vstg", ri)], writes=[("vfs", ri)])
                                em.op('dve', lambda e: e.tensor_copy(out=vd[:, :, 1, 64:128], in_=vsrc),
                                      reads=[("vstg", ri)], writes=[("vfs", ri)])

                            kp = lambda g, e_, ri=ri: KDS[64 * e_:64 * e_ + 64, ri, g, :]
                            vp = lambda g, e_, ri=ri: VFS[:, ri, 2 * g + e_, :]
                            prev = (kp, vp, 128, [("kds", ri)], [("vfs", ri), ("vfsz",)])
                            kc_ = lambda g, e_: KD[64 * e_:64 * e_ + 64, g, TPT:TPT + NSAMP]
                            vc = lambda g, e_: VFN[:, 2 * g + e_, :]
                            cur = (kc_, vc, NSAMP, [("kd", 0), ("kd", 1)], [("vfn",), ("vfnz",)])
                            q0 = c0 + NST * b
                            for g in range(2):
                                gen = attn_unit(4 * g, 4, g, q0, NST, prev, cur,
                                                MASKSP[:, g, :, :], MASKS[:, b, g, :, :],
                                                ATT[:, 4 * g:4 * g + 4, NST * b:NST * b + NST], small=True)
                                if g == 0:
                                    gen = with_prelude(pre, gen)
                                post = (lambda c0=c0: merge_part(c0, NSAMP)) if (g == 1 and b == NSQ - 1) else None
                                items.append((gen, post, 3))
                if items:
                    sgu_part(blocks[0][0], blocks[0][1], 0, blocks[0][2])
                pend = []

                def finish_one():
                    gen, post, _ = pend.pop(0)
                    for _ in gen:
                        pass
                    if post is not None:
                        post()

                for it in items:
                    while len(pend) > it[2] - 1:
                        finish_one()
                    next(it[0])
                    pend.append(it)
                while pend:
                    finish_one()
                if t < NTILES - 1:
                    em.op('dve', lambda e: e.tensor_copy(out=KDC[:, l, :, :], in_=KD[:, :, TPT - 128:TPT]),
                          reads=[("kd", 0), ("kd", 1)], writes=[("kdc", l)])
                    em.op('dve', lambda e: e.tensor_copy(out=VFC[:, l, :, :], in_=VFB[:, NBLK - 1, :, :]),
                          reads=[("vf", NBLK - 1), ("vfz",), ("vfcz",)], writes=[("vfc", l)])

                ckpt(f"attn_{t}_{l}")
                for s in range(8):
                    w, wk = w_acquire('o', s, l)
                    if s == 0:
                        stats_begin()
                    for mm in range(2):
                        m = 2 * s + mm
                        ps, pk = next_ps()
                        proj_fm(w, wk, mm * 128, ps, pk)
                        em.op('dve', lambda e, m=m, ps=ps: e.tensor_tensor(out=XT[:, m, 0:TT], in0=XT[:, m, 0:TT],
                                                                          in1=ps[:, 0:TT], op=ALU.add),
                              reads=[pk, ("xt", m)], writes=[("xt", m)])
                        stats_push(m)
                stats_finish(l, G_FFN)
                em.barrier()
                st2.close()
            ckpt(f"mixer_{t}_{l}")
            with contextlib.ExitStack() as st:
                HB = st.enter_context(sbt("HB", [128, 2, 2, 2 + TPT], F32))
                T0, TA = ZQ, RS
                T1 = st.enter_context(sbt("T1", [128, 2, TTMAX], F32))
                SGB = st.enter_context(sbt("SGB", [128, 2, TTMAX], F32))
                ACTB = st.enter_context(sbt("ACTB", [128, 15, TTMAX], BF16))
                if has_s:
                    CS = st.enter_context(sbt("CS", [128, 2 * NFC, NSQ, 2], F32))
                    HS = st.enter_context(sbt("HS", [128, 2, NSQ, 6], F32))
                    em.dma('sp', CS[:].rearrange("p c b j -> p (c b j)"), cs_d[l].rearrange("p c b j -> p (c b j)"),
                           writes=[("cs",)])
                for g, (f0, f1) in enumerate(FG):
                    nk = f1 - f0
                    for j in range(f0, f1):
                        w, wk = w_acquire('up', j, l)
                        rj = j % 2
                        for half in range(2):
                            cidx = j + NFC * half
                            ps, pk = next_ps()
                            proj_fm(w, wk, half * 128, ps, pk)
                            hb = HB[:, rj, half, :]
                            hk = ("hb", rj, half)
                            cw = lambda i, cidx=cidx: CWB[:, l, cidx, i:i + 1]
                            em.op('act', lambda e, hb=hb, ps=ps: e.activation(out=hb[:, 2:2 + TPT], in_=ps[:, 0:TPT], func=AF.Copy),
                                  reads=[pk], writes=[hk])
                            em.op('pool', lambda e, hb=hb, cidx=cidx: e.tensor_copy(out=hb[:, 0:2], in_=CONVC[:, l, cidx, :]),
                                  reads=[("convc",)], writes=[hk])
                            em.op('pool', lambda e, hb=hb, cidx=cidx: e.tensor_copy(out=CONVC[:, l, cidx, :], in_=hb[:, TPT:TPT + 2]),
                                  reads=[hk], writes=[("convc",)])
                            t0 = T0[:, half, :]
                            t1 = T1[:, half, :]
                            tfin = TA[:, rj, :] if half == 0 else T0[:, half, :]
                            em.op('act', lambda e, hb=hb, t0=t0, cw=cw: e.activation(
                                out=t0[:, 0:TPT], in_=hb[:, 0:TPT], func=AF.Identity, scale=cw(0), bias=cw(3)),
                                reads=[hk, ("cwb",)], writes=[("zq", half)])
                            em.op('dve', lambda e, hb=hb, t0=t0, t1=t1, cw=cw: e.scalar_tensor_tensor(
                                out=t1[:, 0:TPT], in0=hb[:, 1:1 + TPT], scalar=cw(1), in1=t0[:, 0:TPT],
                                op0=ALU.mult, op1=ALU.add),
                                reads=[hk, ("zq", half), ("cwb",)], writes=[("rs0", half)])
                            fkey = ("rs", rj) if half == 0 else ("zq", half)
                            em.op('dve', lambda e, hb=hb, t1=t1, tfin=tfin, cw=cw: e.scalar_tensor_tensor(
                                out=tfin[:, 0:TPT], in0=hb[:, 2:2 + TPT], scalar=cw(2), in1=t1[:, 0:TPT],
                                op0=ALU.mult, op1=ALU.add),
                                reads=[hk, ("rs0", half), ("cwb",)], writes=[fkey])
                            if has_s:
                                hs = HS[:, half, :, :]
                                sk = ("hs", half)
                                em.op('act', lambda e, hs=hs, ps=ps: e.activation(
                                    out=hs[:, :, 2:6], in_=ps[:, TPT:TT].rearrange("p (b t) -> p b t", b=NSQ), func=AF.Copy),
                                    reads=[pk], writes=[sk])
                                em.op('dve', lambda e, hs=hs, cidx=cidx: e.tensor_copy(out=hs[:, :, 0:2], in_=CS[:, cidx, :, :]),
                                      reads=[("cs",)], writes=[sk])
                                em.op('dve', lambda e, hs=hs, cidx=cidx: e.tensor_copy(out=CS[:, cidx, :, :], in_=hs[:, :, 4:6]),
                                      reads=[sk], writes=[("cs",)])
                                v3 = lambda ap: ap[:, TPT:TT].rearrange("p (b t) -> p b t", b=NSQ)
                                em.op('act', lambda e, hs=hs, t0=t0, cw=cw, v3=v3: e.activation(
                                    out=v3(t0), in_=hs[:, :, 0:4], func=AF.Identity, scale=cw(0), bias=cw(3)),
                                    reads=[sk, ("cwb",)], writes=[("zq", half)])
                                em.op('dve', lambda e, hs=hs, t0=t0, t1=t1, cw=cw, v3=v3: e.scalar_tensor_tensor(
                                    out=v3(t1), in0=hs[:, :, 1:5], scalar=cw(1), in1=v3(t0), op0=ALU.mult, op1=ALU.add),
                                    reads=[sk, ("zq", half), ("cwb",)], writes=[("rs0", half)])
                                em.op('dve', lambda e, hs=hs, t1=t1, tfin=tfin, cw=cw, v3=v3: e.scalar_tensor_tensor(
                                    out=v3(tfin), in0=hs[:, :, 2:6], scalar=cw(2), in1=v3(t1), op0=ALU.mult, op1=ALU.add),
                                    reads=[sk, ("rs0", half), ("cwb",)], writes=[fkey])
                            if half == 1:
                                em.op('act', lambda e, tfin=tfin, rj=rj: e.activation(out=SGB[:, rj, 0:TT], in_=tfin[:, 0:TT],
                                                                                     func=AF.Silu),
                                      reads=[fkey], writes=[("sgb", rj)])
                        em.op('dve', lambda e, rj=rj, j=j, f0=f0: e.tensor_tensor(
                            out=ACTB[:, j - f0, 0:TT], in0=TA[:, rj, 0:TT], in1=SGB[:, rj, 0:TT], op=ALU.mult),
                            reads=[("rs", rj), ("sgb", rj)], writes=[("actb", j - f0)])
                    akeys = [("actb", i) for i in range(nk)]
                    for m in range(16):
                        w, wk = w_acquire('dn', (g, m), l)
                        lastg = (g == len(FG) - 1)
                        ovl = (l == 0 and lastg)
                        nxt = (l == DEPTH - 1 and lastg and t < NTILES - 1)
                        if ovl and m == 0:
                            stats_begin()
                        if nxt and m == 0:
                            TTn = TTMAX if t + 1 == NTILES - 1 else TPT
                            stats_begin(TTn, [(0, 512)] + ([(512, NSAMP)] if t + 1 == NTILES - 1 else []))
                        ps, pk = next_ps()
                        proj_fm(w, wk, 0, ps, pk, kc=nk, rhs=ACTB, rkeys=akeys, stride=128, split_last=(m == 0))
                        em.op('dve', lambda e, m=m, ps=ps: e.tensor_tensor(out=XT[:, m, 0:TT], in0=XT[:, m, 0:TT],
                                                                          in1=ps[:, 0:TT], op=ALU.add),
                              reads=[pk, ("xt", m)], writes=[("xt", m)])
                        if ovl:
                            stats_push(m)
                        if nxt:
                            em.dma('sp', yT_d[:, m, t * TPT:(t + 1) * TPT], XT[:, m, 0:TPT], reads=[("xt", m)])
                            for mm_ in ([m - 4] if m >= 4 else []) + (list(range(m - 3, m + 1)) if m == 15 else []):
                                em.dma('sp', XT[:, mm_, 0:TPT], xT_d[:, mm_, (t + 1) * TPT:(t + 2) * TPT],
                                       writes=[("xt", mm_)])
                                stats_push(mm_, delay=4)
                if has_s:
                    em.dma('sp', cvp_d[l].rearrange("p c j -> p (c j)"), CONVC[:, l].rearrange("p c j -> p (c j)"),
                           reads=[("convc",)])
                    em.dma('sp', cvs_d[l].rearrange("p c b j -> p (c b j)"), CS[:].rearrange("p c b j -> p (c b j)"),
                           reads=[("cs",)])
                em.barrier()
            ckpt(f"ffn_{t}_{l}")
        if t == NTILES - 1:
            em.dma('sp', yT_d[:, :, t * TPT:(t + 1) * TPT], XT[:, :, 0:TPT], reads=xkeys)
            em.dma('sp', yT_d[:, :, S:S + NSAMP], XT[:, :, TPT:TT], reads=xkeys)

    try:
        ckpt("consts")
        for t in range(NTILES):
            run_tile(t)
    except StopBuild:
        pass
    em.finish()
    return nc


def _arr_cols(W, cols):
    Wc = W[:, cols]
    kc = W.shape[0] // 128
    return Wc.reshape(kc, 128, Wc.shape[1]).transpose(1, 0, 2).reshape(128, kc * Wc.shape[1])


def _prep_weights(w_in, w_o, w_up, w_down):
    out = np.empty((DEPTH, 128, EPL), np.float32)
    ar = np.arange
    for l in range(DEPTH):
        for (kind, idx, off, ne) in SLOTS:
            if kind == 'vs':
                a = _arr_cols(w_in[l], ar(2304 + 256 * idx, 2304 + 256 * idx + 256))
            elif kind == 'u':
                a = _arr_cols(w_in[l], ar(1280 + 256 * idx, 1280 + 256 * idx + 256))
            elif kind == 'q':
                a = _arr_cols(w_in[l], ar(256 * idx, 256 * idx + 256))
            elif kind == 'kd':
                k0 = ar(1024, 1088)
                k1 = ar(1088, 1152)
                a = _arr_cols(w_in[l], np.concatenate([k0, k0, k1, k1]))
            elif kind == 'kv':
                a = _arr_cols(w_in[l], ar(1024, 1280))
            elif kind == 'o':
                a = _arr_cols(w_o[l], ar(256 * idx, 256 * idx + 256))
            elif kind == 'up':
                a = _arr_cols(w_up[l], np.concatenate([ar(128 * idx, 128 * idx + 128),
                                                       ar(DFF + 128 * idx, DFF + 128 * idx + 128)]))
            elif kind == 'dn':
                g, m = idx
                f0, f1 = FG[g]
                a = _arr_cols(w_down[l][f0 * 128:f1 * 128], ar(128 * m, 128 * m + 128))
            out[l, :, off:off + ne] = a
    return out


def _consts():
    h = np.arange(1, NH + 1, dtype=np.float32)
    slopes = np.exp2(-8.0 * h / NH).astype(np.float32)
    j = np.arange(128)[:, None]
    i = np.arange(128)[None, :]
    maskp = np.zeros((128, 4, 2, 2, 2, 128), np.float32)
    masks = np.zeros((NSAMP, NSQ, 2, 2, 4, NST), np.float32)
    masksp = np.zeros((128, 2, 2, 4, NST), np.float32)
    dprev = np.maximum(i - j + 128, 0).astype(np.float32)
    dcur = np.maximum(i - j, 0).astype(np.float32)
    for mp in range(4):
        for e_ in range(2):
            for mi in range(2):
                sl = slopes[4 * mp + 2 * mi + e_]
                maskp[:, mp, e_, 0, mi, :] = np.where(j > i, np.exp(-sl * dprev), 0.0)
                maskp[:, mp, e_, 1, mi, :] = np.where(j <= i, np.exp(-sl * dcur), 0.0)
                gg, ml = mp // 2, 2 * (mp % 2) + mi
                masksp[:, gg, e_, ml, :] = maskp[:, mp, e_, 0, mi, 0:NST]
                for jj in range(NSAMP):
                    b, tj = jj // NST, jj % NST
                    for ti in range(NST):
                        if tj <= ti:
                            masks[jj, b, gg, e_, ml, ti] = np.exp(-sl * float(ti - tj))
    maskp = maskp.reshape(128, 4, 2, 2, 256)
    masks = masks.reshape(NSAMP, NSQ, 2, 2, 4 * NST)
    masksp = masksp.reshape(128, 2, 2, 4 * NST)
    tril = (j <= i).astype(np.float32)
    js = np.arange(NSAMP)[:, None]
    is_ = np.arange(NSAMP)[None, :]
    trils = ((js // NST == is_ // NST) & (js % NST <= is_ % NST)).astype(np.float32)
    cmat = np.zeros((128, 4, 128), np.float32)
    cmat[:, 0, :] = 1.0
    cmat[:, 1, :] = ((np.arange(128)[:, None] // 64) == (np.arange(128)[None, :] // 64)).astype(np.float32)
    cmat[:, 2, 0:64] = 1.0
    cmat[:, 3, 64:128] = 1.0
    return maskp, masks, masksp, tril, trils, cmat


_NC_CACHE = {}
_DEBUG = {}


def kernel(x_prompt, x_sample, cache_k_win, cache_v_win, state_ffn_conv,
           norm_mix_g, w_in, q_norm_g, k_norm_g, attn_sinks, sgu_ln_g, sgu_ln_b,
           sgu_w, sgu_b, out_norm_att_g, out_norm_sgu_g, w_o, norm_ffn_g,
           w_up, conv_w, conv_b, w_down):
    f32 = np.float32
    A = lambda a: np.asarray(a, dtype=f32)
    x_prompt, x_sample = A(x_prompt), A(x_sample)
    cache_k_win, cache_v_win, state_ffn_conv = A(cache_k_win), A(cache_v_win), A(state_ffn_conv)
    w_in, w_o, w_up, w_down = A(w_in), A(w_o), A(w_up), A(w_down)

    wts = _prep_weights(w_in, w_o, w_up, w_down)
    maskp, masks, masksp, tril, trils, cmat = _consts()

    def cvec(v, n):
        return A(v).reshape(DEPTH, n, 128).transpose(2, 0, 1)

    gvec = np.zeros((128, DEPTH, 59), f32)
    gvec[64, :, 58] = 1.0
    gvec[:, :, 0:16] = cvec(norm_mix_g, 16)
    gvec[:, :, 16:32] = cvec(norm_ffn_g, 16)
    gvec[:, :, 32:40] = cvec(out_norm_att_g, 8)
    gvec[:, :, 40:48] = cvec(out_norm_sgu_g, 8)
    sk = A(attn_sinks)
    p = np.arange(128)
    for m in range(8):
        gvec[:, :, 48 + m] = sk[:, 2 * m + p // 64].T
    skrow = np.zeros((DEPTH, 8, 128), f32)
    for m in range(8):
        skrow[:, m, :] = sk[:, 2 * m + p // 64]
    skrow = skrow.reshape(1, DEPTH * 8 * 128)
    gvec[:, :, 56] = A(q_norm_g)[:, p % 64].T
    gvec[:, :, 57] = A(k_norm_g)[:, p % 64].T
    cwb = np.zeros((128, DEPTH, 2 * NFC, 4), f32)
    cwv = A(conv_w).reshape(DEPTH, 3, 2 * NFC, 128)
    cwb[:, :, :, 0:3] = cwv.transpose(3, 0, 2, 1)
    cwb[:, :, :, 3] = A(conv_b).reshape(DEPTH, 2 * NFC, 128).transpose(2, 0, 1)
    lngb = np.zeros((DEPTH, 2, 128, 1024), f32)
    lngb[:, 0] = A(sgu_ln_g)[:, None, :]
    lngb[:, 1] = A(sgu_ln_b)[:, None, :]
    gkb = np.broadcast_to(np.tile(A(k_norm_g), (1, 2))[:, None, :], (DEPTH, 128, 128)).copy()
    wsT = np.ascontiguousarray(A(sgu_w).transpose(0, 3, 1, 2))
    wsS = np.zeros((DEPTH, NSAMP, 8, NSAMP), f32)
    for b in range(NSQ):
        wsS[:, NST * b:NST * b + NST, :, NST * b:NST * b + NST] = wsT[:, 0:NST, :, 0:NST]
    bsr = np.ascontiguousarray(A(sgu_b))
    bss = np.ascontiguousarray(np.tile(A(sgu_b)[:, :, 0:NST], (1, 1, NSQ)))

    in_maps = []
    for c in range(NCORES):
        xfull = np.concatenate([x_prompt[c], x_sample[NSQ * c:NSQ * (c + 1)].reshape(NSAMP, D)], axis=0)
        xT = np.ascontiguousarray(xfull.reshape(NTOK, ND, 128).transpose(2, 1, 0))
        ck = cache_k_win[:, NSQ * c:NSQ * (c + 1)]
        cv = cache_v_win[:, NSQ * c:NSQ * (c + 1)]
        ckT = np.ascontiguousarray(ck.transpose(0, 1, 3, 4, 2))
        cs = state_ffn_conv[:, NSQ * c:NSQ * (c + 1)]
        cs_l = np.ascontiguousarray(cs.reshape(DEPTH, NSQ, 2, 2 * NFC, 128).transpose(0, 4, 3, 1, 2))
        in_maps.append({
            "xT": xT, "wts": wts, "ckT": ckT,
            "ck": np.ascontiguousarray(ck.reshape(DEPTH, NSQ, 128, 128)),
            "cv": np.ascontiguousarray(cv.reshape(DEPTH, NSQ, 128, 128)),
            "cs": cs_l, "gvec": gvec, "skrow": skrow, "cwb": cwb, "lngb": lngb, "gkb": gkb, "wsT": wsT, "wsS": wsS,
            "bsr": bsr, "bss": bss, "maskp": maskp, "masks": masks, "masksp": masksp, "tril": tril, "trils": trils, "cmat": cmat,
        })

    if _DEBUG.get("prep_only"):
        return in_maps
    if "nc" not in _NC_CACHE:
        _NC_CACHE["nc"] = build_program()
    nc = _NC_CACHE["nc"]
    res = run_bass_kernel_spmd(nc, in_maps, core_ids=list(range(NCORES)))
    R = res.results

    y_prompt = np.empty((NCORES, S, D), f32)
    y_sample = np.empty((NCORES * NSQ, NST, D), f32)
    kwp = np.empty((DEPTH, NCORES, 128, 2, HD), f32)
    vwp = np.empty_like(kwp)
    cvp = np.empty((DEPTH, NCORES, 2, 2 * DFF), f32)
    kws = np.empty((DEPTH, NCORES * NSQ, 128, 2, HD), f32)
    vws = np.empty_like(kws)
    cvs = np.empty((DEPTH, NCORES * NSQ, 2, 2 * DFF), f32)
    sgv = np.empty((DEPTH, NCORES * NSQ, NST, 8, 128), f32)
    for c in range(NCORES):
        r = R[c]
        y = np.asarray(r["yT"]).transpose(2, 1, 0).reshape(NTOK, D)
        y_prompt[c] = y[:S]
        y_sample[NSQ * c:NSQ * (c + 1)] = y[S:].reshape(NSQ, NST, D)
        kwp[:, c] = np.asarray(r["kwp"]).reshape(DEPTH, 128, 2, HD)
        vwp[:, c] = np.asarray(r["vwp"]).reshape(DEPTH, 128, 2, HD)
        cvp[:, c] = np.asarray(r["cvp"]).transpose(0, 3, 2, 1).reshape(DEPTH, 2, 2 * DFF)
        kws[:, NSQ * c:NSQ * (c + 1)] = np.asarray(r["kws"]).reshape(DEPTH, NSQ, 128, 2, HD)
        vws[:, NSQ * c:NSQ * (c + 1)] = np.asarray(r["vws"]).reshape(DEPTH, NSQ, 128, 2, HD)
        cvs[:, NSQ * c:NSQ * (c + 1)] = np.asarray(r["cvs"]).transpose(0, 3, 4, 2, 1).reshape(DEPTH, NSQ, 2, 2 * DFF)
        sgv[:, NSQ * c:NSQ * (c + 1)] = np.asarray(r["sgv"]).reshape(DEPTH, NSQ, NST, 8, 128)
    return (y_prompt, y_sample, kwp, vwp, cvp, kws, vws, cvs, sgv)
```

```python
import contextlib
import numpy as np
import concourse.bass as bass
import concourse.mybir as mybir
from concourse.bass_utils import run_bass_kernel_spmd

F32 = mybir.dt.float32
BF16 = mybir.dt.bfloat16
AF = mybir.ActivationFunctionType
ALU = mybir.AluOpType
AX = mybir.AxisListType

D = 2048
ND = 16
HD = 64
NH = 16
DFF = 5504
NFC = 43
S = 2048
TPT = 512
NTILES = 4
NBLK = 4
NSQ = 16
NST = 4
NSAMP = 64
TTMAX = TPT + NSAMP
NTOK = S + NSAMP
EPS = 1e-6
NSLOT = 5
SLOT_ELEMS = 4096
FG = [(0, 15), (15, 29), (29, 43)]
NCORES = 8
DEPTH = 2


def weight_layout():
    slots = []
    off = 0

    def add(kind, idx, ne):
        nonlocal off
        slots.append((kind, idx, off, ne))
        off += ne

    for s in range(4):
        add('vs', s, 16 * 256)
    for s in range(4):
        add('u', s, 16 * 256)
    for s in range(4):
        add('q', s, 16 * 256)
    add('kd', 0, 16 * 256)
    add('kv', 0, 16 * 256)
    for s in range(8):
        add('o', s, 16 * 256)
    for g, (f0, f1) in enumerate(FG):
        for j in range(f0, f1):
            add('up', j, 16 * 256)
        for m in range(16):
            add('dn', (g, m), (f1 - f0) * 128)
    return slots, off


SLOTS, EPL = weight_layout()


class DSem:
    def __init__(self, sh, sid):
        self.sh = sh
        self.sid = sid
        self.val = 0


class Em:
    def __init__(self, nc):
        self.nc = nc
        self.eng = {'pe': nc.tensor, 'act': nc.scalar, 'dve': nc.vector, 'pool': nc.gpsimd, 'sp': nc.sync}
        self.sem = {}
        self.cnt = {}
        self.seen = {e: {} for e in self.eng}
        self.lw = {}
        self.rd = {}
        self.nsem = 0
        self.floor = []
        self.ring = []
        self.ring_i = 0
        self.all_dsems = []
        self.new_epoch()

    def _new_sem(self, name):
        self.nsem += 1
        sh = self.nc.alloc_semaphore(f"{name}_{self.nsem}")
        return sh, self.nsem

    def new_dsem(self, name):
        sh, sid = self._new_sem(name)
        d = DSem(sh, sid)
        self.all_dsems.append(d)
        return d

    def new_epoch(self):
        for e in ('pe', 'act', 'dve', 'pool'):
            self.sem[e] = self._new_sem(e)
            self.cnt[e] = 0

    def _wait(self, eng, deps):
        need = {}
        for d in list(deps) + self.floor:
            if d is None:
                continue
            sid, sh, v, src = d
            if src == eng and eng == 'pe':
                continue
            if v <= self.seen[eng].get(sid, 0):
                continue
            if sid not in need or need[sid][1] < v:
                need[sid] = (sh, v)
        for sid, (sh, v) in need.items():
            self.eng[eng].wait_ge(sh, v)
            self.seen[eng][sid] = v

    def _deps(self, reads, writes):
        deps = []
        for k in reads:
            deps.append(self.lw.get(k))
        for k in writes:
            deps.append(self.lw.get(k))
            deps.extend(self.rd.get(k, {}).values())
        return deps

    def _record(self, tok, reads, writes):
        for k in writes:
            self.lw[k] = tok
            self.rd[k] = {}
        for k in reads:
            r = self.rd.setdefault(k, {})
            o = r.get(tok[0])
            if o is None or o[2] < tok[2]:
                r[tok[0]] = tok

    def op(self, eng, fn, reads=(), writes=()):
        self._wait(eng, self._deps(reads, writes))
        ins = fn(self.eng[eng])
        sh, sid = self.sem[eng]
        self.cnt[eng] += 1
        ins.then_inc(sh, 1)
        tok = (sid, sh, self.cnt[eng], eng)
        self._record(tok, reads, writes)
        return tok

    def pe_group(self, mms, reads=(), writes=()):
        self._wait('pe', self._deps(reads, writes))
        ins = None
        for (o, l, r, st, sp) in mms:
            ins = self.nc.tensor.matmul(o, lhsT=l, rhs=r, start=st, stop=sp)
        sh, sid = self.sem['pe']
        self.cnt['pe'] += 1
        ins.then_inc(sh, 1)
        tok = (sid, sh, self.cnt['pe'], 'pe')
        self._record(tok, reads, writes)
        return tok

    def dma(self, q, out, in_, reads=(), writes=(), ds=None):
        if ds is None:
            if not self.ring:
                self.ring = [self.new_dsem("dr") for _ in range(12)]
                self.ring_p = [self.new_dsem("dp") for _ in range(6)]
            rg = self.ring_p if q == 'pool' else self.ring
            ds = rg[self.ring_i % len(rg)]
            self.ring_i += 1
        deps = self._deps(reads, writes)
        if ds.val > 0:
            deps.append((ds.sid, ds.sh, ds.val, 'dma'))
        self._wait(q, deps)
        ins = self.eng[q].dma_start(out=out, in_=in_)
        ds.val += 16
        ins.then_inc(ds.sh, 16)
        tok = (ds.sid, ds.sh, ds.val, 'dma')
        self._record(tok, reads, writes)
        return tok

    def barrier(self, exclude=()):
        ex = {(tk[0], tk[2]) for tk in exclude}
        fl = []
        for e in ('pe', 'act', 'dve', 'pool'):
            sh, sid = self.sem[e]
            if self.cnt[e] > 0:
                fl.append((sid, sh, self.cnt[e], 'bar'))
        for d in self.all_dsems:
            v = d.val
            while v > 0 and (d.sid, v) in ex:
                v -= 16
            if v > 0:
                fl.append((d.sid, d.sh, v, 'bar'))
        self.floor = fl

    def finish(self):
        self.barrier()
        self._wait('sp', [])


class StopBuild(Exception):
    pass


def build_program(stop=None):
    nc = bass.Bass("TRN2", target_bir_lowering=False)
    em = Em(nc)

    def ckpt(name):
        if stop is not None and stop == name:
            raise StopBuild(name)


    def din(name, shape, dt=F32):
        return nc.dram_tensor(name, list(shape), dt, kind="ExternalInput").ap()

    def dout(name, shape, dt=F32):
        return nc.dram_tensor(name, list(shape), dt, kind="ExternalOutput").ap()

    xT_d = din("xT", [128, ND, NTOK])
    wts_d = din("wts", [DEPTH, 128, EPL])
    ckT_d = din("ckT", [DEPTH, NSQ, 2, HD, 128])
    ck_d = din("ck", [DEPTH, NSQ, 128, 128])
    cv_d = din("cv", [DEPTH, NSQ, 128, 128])
    cs_d = din("cs", [DEPTH, 128, 2 * NFC, NSQ, 2])
    gvec_d = din("gvec", [128, DEPTH, 59])
    skrow_d = din("skrow", [1, DEPTH * 8 * 128])
    cwb_d = din("cwb", [128, DEPTH, 2 * NFC, 4])
    lngb_d = din("lngb", [DEPTH, 2, 128, 1024])
    gkb_d = din("gkb", [DEPTH, 128, 128])
    wsT_d = din("wsT", [DEPTH, 128, 8, 128])
    wsS_d = din("wsS", [DEPTH, NSAMP, 8, NSAMP])
    bsr_d = din("bsr", [DEPTH, 8, 128])
    bss_d = din("bss", [DEPTH, 8, NSAMP])
    maskp_d = din("maskp", [128, 4, 2, 2, 256])
    masks_d = din("masks", [NSAMP, NSQ, 2, 2, 4 * NST])
    masksp_d = din("masksp", [128, 2, 2, 4 * NST])
    tril_d = din("tril", [128, 128])
    trils_d = din("trils", [NSAMP, NSAMP])
    cmat_d = din("cmat", [128, 4, 128])

    yT_d = dout("yT", [128, ND, NTOK])
    kwp_d = dout("kwp", [DEPTH, 128, 128])
    vwp_d = dout("vwp", [DEPTH, 128, 128])
    cvp_d = dout("cvp", [DEPTH, 128, 2 * NFC, 2])
    kws_d = dout("kws", [DEPTH, NSQ, 128, 128])
    vws_d = dout("vws", [DEPTH, NSQ, 128, 128])
    cvs_d = dout("cvs", [DEPTH, 128, 2 * NFC, NSQ, 2])
    sgv_d = dout("sgv", [DEPTH, NSAMP, 1024])

    sb = nc.alloc_sbuf_tensor
    XT = sb("XT", [128, ND, TTMAX], F32)
    XN = sb("XN", [128, ND, TTMAX], BF16)
    WS = [sb(f"WS{i}", [128, SLOT_ELEMS], BF16) for i in range(NSLOT)]
    CMAT = sb("CMAT", [128, 4, 128], BF16)
    ONES = CMAT[:, 0, :]
    BD64 = CMAT[:, 1, :]
    MASKP = sb("MASKP", [128, 4, 2, 2, 256], BF16)
    MASKS = sb("MASKS", [NSAMP, NSQ, 2, 2, 4 * NST], BF16)
    MASKSP = sb("MASKSP", [128, 2, 2, 4 * NST], BF16)
    EPSC = sb("EPSC", [128, 1], F32)
    GV_ = sb("GVEC", [128, DEPTH, 59], F32)
    ESK = sb("ESK", [128, DEPTH, 8], F32)
    CWB = sb("CWB", [128, DEPTH, 2 * NFC, 4], F32)
    WST = sb("WST", [128, DEPTH, 8, 128], BF16)
    WSS = sb("WSS", [NSAMP, DEPTH, 8, NSAMP], BF16)
    BSR = sb("BSR", [128, DEPTH, 8, 128], BF16)
    KDC = sb("KDC", [128, DEPTH, 2, 128], BF16)
    VFB = sb("VFB", [128, NBLK, 4, 128], BF16)
    VFC = sb("VFC", [128, DEPTH, 4, 128], BF16)
    VFN = sb("VFN", [NSAMP, 4, 128], BF16)
    CONVC = sb("CONVC", [128, DEPTH, 2 * NFC, 2], F32)
    KDS = sb("KDS", [128, 2, 2, 128], BF16)
    VFS = sb("VFS", [128, 2, 4, 128], BF16)
    ZQ = sb("ZQ", [128, 2, TTMAX], F32)
    SQ = sb("SQ", [128, 2, TTMAX], BF16)
    RS = sb("RS", [128, 2, TTMAX], F32)
    VS = sb("VS", [128, NBLK + 1, 1024], BF16)

    PS = [nc.alloc_psum_tensor(f"ps{r}", [128, 1024], F32) for r in range(4)]
    ps_i = [0]

    ps_reserved = set()

    def next_ps():
        while True:
            r = ps_i[0] % 4
            ps_i[0] += 1
            if r not in ps_reserved:
                return PS[r], ("ps", r)

    def gcol(l, off, c):
        return GV_[:, l, off + c: off + c + 1]

    G_MIX, G_FFN, G_OA, G_OS, G_SK, G_Q, G_K = 0, 16, 32, 40, 48, 56, 57

    wsems = [em.new_dsem(f"ws{i}") for i in range(NSLOT)]
    wseq = [(l, si) for _t in range(NTILES) for l in range(DEPTH) for si in range(len(SLOTS))]
    wstate = {'issue': 0, 'use': 0, 'rel': 0}

    def w_issue():
        i = wstate['issue']
        if i >= len(wseq):
            return
        l, si = wseq[i]
        kind, idx, off, ne = SLOTS[si]
        b = i % NSLOT
        em.dma('pool', WS[b][:, 0:ne], wts_d[l, :, off:off + ne], writes=[("ws", b)], ds=wsems[b])
        wstate['issue'] += 1

    def w_acquire(kind, idx, l, hold=False):
        i = wstate['use']
        wstate['use'] += 1
        if not hold:
            wstate['rel'] = i
        while wstate['issue'] < min(len(wseq), wstate['rel'] + NSLOT):
            w_issue()
        ll, si = wseq[i]
        assert ll == l and SLOTS[si][0] == kind and SLOTS[si][1] == idx, (SLOTS[si], kind, idx, l, ll)
        assert i < wstate['issue']
        b = i % NSLOT
        return WS[b], ("ws", b)

    for _ in range(NSLOT):
        w_issue()

    cs_ = em.new_dsem("cst")
    with contextlib.ExitStack() as st:
        tmpf = st.enter_context(nc.sbuf_tensor("tmpf", [128, 4096], F32))
        tmpg = st.enter_context(nc.sbuf_tensor("tmpg", [128, 4096], F32))

        def ld(out, in_, keys):
            em.dma('sp', out, in_, writes=keys)

        ld(GV_[:], gvec_d, [("gvec",)])
        ld(CWB[:], cwb_d, [("cwb",)])
        ld(tmpf[:, 0:512], cmat_d.rearrange("p a b -> p (a b)"), [("tmpf",)])
        em.op('dve', lambda e: e.tensor_copy(out=CMAT[:].rearrange("p a b -> p (a b)"), in_=tmpf[:, 0:512]),
              reads=[("tmpf",)], writes=[("cmat",)])
        ld(tmpg[:, 0:4096], maskp_d.rearrange("p a b c q -> p (a b c q)"), [("tmpg",)])
        em.op('dve', lambda e: e.tensor_copy(out=MASKP[:].rearrange("p a b c q -> p (a b c q)"), in_=tmpg[:, 0:4096]),
              reads=[("tmpg",)], writes=[("maskp",)])
        ld(tmpf[0:NSAMP, 0:1024], masks_d.rearrange("p b a c q -> p (b a c q)"), [("tmpf",)])
        em.op('dve', lambda e: e.tensor_copy(out=MASKS[:].rearrange("p b a c q -> p (b a c q)"), in_=tmpf[0:NSAMP, 0:1024]),
              reads=[("tmpf",)], writes=[("masks",)])
        ld(tmpf[:, 1024:1088], masksp_d.rearrange("p g e q -> p (g e q)"), [("tmpf2",)])
        em.op('dve', lambda e: e.tensor_copy(out=MASKSP[:].rearrange("p g e q -> p (g e q)"), in_=tmpf[:, 1024:1088]),
              reads=[("tmpf2",)], writes=[("masks",)])
        TR = st.enter_context(nc.sbuf_tensor("TR", [128, 128], F32))
        TRS = st.enter_context(nc.sbuf_tensor("TRS", [NSAMP, NSAMP], F32))
        ld(TR[:], tril_d, [("tr",)])
        ld(TRS[:], trils_d, [("trs",)])
        for l in range(DEPTH):
            ld(tmpg[:, 0:1024], wsT_d[l].rearrange("p h t -> p (h t)"), [("tmpg",)])
            em.op('dve', lambda e, l=l: e.tensor_tensor(
                out=WST[:, l, :, :], in0=tmpg[:, 0:1024].rearrange("p (h t) -> p h t", h=8),
                in1=TR[:].rearrange("p (o t) -> p o t", o=1).broadcast_to([128, 8, 128]), op=ALU.mult),
                reads=[("tmpg",), ("tr",)], writes=[("wst", l)])
            ld(tmpf[0:NSAMP, 0:512], wsS_d[l].rearrange("p h t -> p (h t)"), [("tmpf",)])
            em.op('dve', lambda e, l=l: e.tensor_tensor(
                out=WSS[:, l, :, :], in0=tmpf[0:NSAMP, 0:512].rearrange("p (h t) -> p h t", h=8),
                in1=TRS[:].rearrange("p (o t) -> p o t", o=1).broadcast_to([NSAMP, 8, NSAMP]), op=ALU.mult),
                reads=[("tmpf",), ("trs",)], writes=[("wss", l)])
            ld(tmpg[0:1, 0:1024], bsr_d[l:l + 1].rearrange("o h t -> o (h t)"), [("tmpg",)])
            em.op('dve', lambda e, l=l: e.tensor_copy(out=BSR[0:1, l, :, :].rearrange("p h t -> p (h t)"),
                                                     in_=tmpg[0:1, 0:1024]),
                  reads=[("tmpg",)], writes=[("bsr", l)])
            ld(tmpf[32:33, 0:512], bss_d[l:l + 1].rearrange("o h t -> o (h t)"), [("tmpf",)])
            em.op('dve', lambda e, l=l: e.tensor_copy(out=BSR[32:33, l, :, 0:NSAMP],
                                                     in_=tmpf[32:33, 0:512].rearrange("p (h t) -> p h t", h=8)),
                  reads=[("tmpf",)], writes=[("bss", l)])
        tmpb = st.enter_context(nc.sbuf_tensor("tmpb", [128, 2048], BF16))
        ld(tmpf[64:65, 0:2048], skrow_d, [("tmpf",)])
        ld(tmpf[65:66, 0:2048], skrow_d, [("tmpf",)])
        A_ = tmpf[64:66, 0:2048]
        B_ = tmpg[64:66, 0:2048]
        C_ = tmpb[64:66, 0:2048]
        em.op('act', lambda e: e.activation(out=B_, in_=A_, func=AF.Exp), reads=[("tmpf",)], writes=[("tmpg",)])
        em.op('dve', lambda e: e.tensor_copy(out=C_, in_=B_), reads=[("tmpg",)], writes=[("tmpb",)])
        em.op('dve', lambda e: e.tensor_copy(out=A_, in_=C_), reads=[("tmpb",)], writes=[("tmpf",)])
        em.op('dve', lambda e: e.tensor_tensor(out=B_, in0=B_, in1=A_, op=ALU.subtract),
              reads=[("tmpf",), ("tmpg",)], writes=[("tmpg",)])
        em.op('dve', lambda e: e.tensor_tensor(out=A_, in0=A_, in1=B_, op=ALU.subtract),
              reads=[("tmpf",), ("tmpg",)], writes=[("tmpf",)])
        em.op('dve', lambda e: e.scalar_tensor_tensor(out=BSR[64:66, :, :, :].rearrange("p l m q -> p (l m q)"),
                                                      in0=A_, scalar=GV_[64:66, 0, 58:59], in1=B_,
                                                      op0=ALU.mult, op1=ALU.add),
              reads=[("tmpf",), ("tmpg",), ("gvec",)], writes=[("esr",)])
        em.dma('sp', XT[:, :, 0:TPT], xT_d[:, :, 0:TPT], writes=[("xt", c) for c in range(ND)])
        em.dma('sp', XT[:, :, TPT:TTMAX], xT_d[:, :, S:S + NSAMP], writes=[("xts",)])
        em.op('dve', lambda e: e.memset(EPSC[:], EPS), writes=[("eps",)])
        em.op('dve', lambda e: e.memset(VFB[:].rearrange("p a c d -> p (a c d)"), 0.0), writes=[("vfz",)])
        em.op('dve', lambda e: e.memset(VFC[:].rearrange("p a c d -> p (a c d)"), 0.0), writes=[("vfcz",)])
        em.op('dve', lambda e: e.memset(VFN[:].rearrange("p c d -> p (c d)"), 0.0), writes=[("vfnz",)])
        em.op('dve', lambda e: e.memset(VFS[:].rearrange("p a c d -> p (a c d)"), 0.0), writes=[("vfsz",)])
        em.op('dve', lambda e: e.memset(CONVC[:].rearrange("p a b c -> p (a b c)"), 0.0), writes=[("convc",)])
        em.op('act', lambda e: e.activation(out=ESK[:, :, :],
                                            in_=GV_[:, :, G_SK:G_SK + 8], func=AF.Exp),
              reads=[("gvec",)], writes=[("esk",)])
        em.barrier()

    def rstd_from_ps(ps, pk, ncols, scale, ring_i, parts=128):
        r1 = RS[0:parts, ring_i, 0:ncols]
        em.op('act', lambda e: e.activation(out=r1, in_=ps, func=AF.Ln, scale=scale, bias=EPSC[0:parts, 0:1]),
              reads=[pk, ("eps",)], writes=[("rs", ring_i)])
        em.op('act', lambda e: e.activation(out=r1, in_=r1, func=AF.Exp, scale=-0.5),
              reads=[("rs", ring_i)], writes=[("rs", ring_i)])
        return r1

    uid = [0]

    def sbt(name, shape, dt):
        uid[0] += 1
        return nc.sbuf_tensor(f"{name}_{uid[0]}", shape, dt)

    stats = {}

    def run_tile(t):
        TT = TTMAX if t == NTILES - 1 else TPT
        has_s = (t == NTILES - 1)
        nts = [(0, 512)] + ([(512, NSAMP)] if has_s else [])
        xkeys = [("xt", c) for c in range(ND)]
        xnkeys = [("xn", c) for c in range(ND)]

        def stats_begin(TTs=None, ntss=None):
            ps, pk = next_ps()
            ps_reserved.add(pk[1])
            stats['ps'], stats['pk'], stats['q'] = ps, pk, []
            stats['TT'], stats['nts'] = (TT if TTs is None else TTs), (nts if ntss is None else ntss)

        def stats_chunk(c):
            ps, pk = stats['ps'], stats['pk']
            TTs, ntss = stats['TT'], stats['nts']
            sq = SQ[:, c % 2, 0:TTs]
            em.op('act', lambda e, c=c, sq=sq: e.activation(out=sq, in_=XT[:, c, 0:TTs], func=AF.Square),
                  reads=[("xt", c)], writes=[("sq", c % 2)])
            em.pe_group([(ps[:, n0:n0 + nn], ONES, SQ[:, c % 2, n0:n0 + nn], c == 0, c == ND - 1)
                         for (n0, nn) in ntss], reads=[("sq", c % 2), ("cmat",)], writes=[pk])

        def stats_push(c, delay=2):
            stats['q'].append(c)
            while len(stats['q']) > delay:
                stats_chunk(stats['q'].pop(0))

        def stats_finish(l, goff):
            while stats['q']:
                stats_chunk(stats['q'].pop(0))
            ps, pk = stats['ps'], stats['pk']
            assert stats['TT'] == TT
            r = rstd_from_ps(ps[:, 0:TT], pk, TT, 1.0 / D, 0)
            ps_reserved.discard(pk[1])
            for c in range(ND):
                em.op('dve', lambda e, c=c: e.scalar_tensor_tensor(
                    out=XN[:, c, 0:TT], in0=XT[:, c, 0:TT], scalar=gcol(l, goff, c), in1=r,
                    op0=ALU.mult, op1=ALU.mult),
                    reads=[("xt", c), ("rs", 0), ("gvec",)], writes=[("xn", c)])

        def rms_to_xn(l, goff):
            stats_begin()
            for c in range(ND):
                stats_chunk(c)
            stats_finish(l, goff)

        def proj_fm(w, wk, col0, ps, pk, kc=ND, rhs=None, rkeys=None, stride=256, split_last=False):
            rkeys = xnkeys if rkeys is None else rkeys
            segs = [(0, kc - 1), (kc - 1, kc)] if split_last else [(0, kc)]
            for (k0, k1) in segs:
                mms = []
                for k in range(k0, k1):
                    for (n0, nn) in nts:
                        mms.append((ps[:, n0:n0 + nn], w[:, k * stride + col0:k * stride + col0 + 128],
                                    (XN if rhs is None else rhs)[:, k, n0:n0 + nn], k == 0, k == kc - 1))
                em.pe_group(mms, reads=[wk] + rkeys[k0:k1], writes=[pk])

        hn_i = [0]

        def headnorm(ps, pk, out_ap, okey, gain):
            i = hn_i[0] % 2
            hn_i[0] += 1
            em.op('act', lambda e: e.activation(out=ZQ[:, i, 0:TT], in_=ps[:, 0:TT], func=AF.Copy),
                  reads=[pk], writes=[("zq", i)])
            em.op('act', lambda e: e.activation(out=SQ[:, i, 0:TT], in_=ps[:, 0:TT], func=AF.Square),
                  reads=[pk], writes=[("sq", i)])
            yield
            ps2, pk2 = next_ps()
            em.pe_group([(ps2[:, n0:n0 + nn], BD64, SQ[:, i, n0:n0 + nn], True, True) for (n0, nn) in nts],
                        reads=[("sq", i), ("cmat",)], writes=[pk2])
            r = rstd_from_ps(ps2[:, 0:TT], pk2, TT, 1.0 / HD, i)
            em.op('dve', lambda e: e.scalar_tensor_tensor(out=out_ap, in0=ZQ[:, i, 0:TT], scalar=gain, in1=r,
                                                          op0=ALU.mult, op1=ALU.mult),
                  reads=[("zq", i), ("rs", i), ("gvec",)], writes=[okey])

        hn_pend = [None]

        def hn_push(gen):
            next(gen)
            hn_flush()
            hn_pend[0] = gen

        def hn_flush():
            if hn_pend[0] is not None:
                for _ in hn_pend[0]:
                    pass
                hn_pend[0] = None

        blocks = [(b * 128, 128, False) for b in range(NBLK)] + ([(TPT, NSAMP, True)] if has_s else [])

        for l in range(DEPTH):
            em.new_epoch()
            ckpt(f"start_{t}_{l}")
            if l == 0 and t == 0:
                rms_to_xn(l, G_MIX)
            else:
                stats_finish(l, G_MIX)
            ckpt(f"rms_{t}_{l}")
            with contextlib.ExitStack() as st:
                QT = st.enter_context(sbt("QT", [128, 8, TTMAX], BF16))
                KD = st.enter_context(sbt("KD", [128, 2, TTMAX], BF16))
                UT = st.enter_context(sbt("UT", [128, 8, TTMAX], BF16))
                KTF = st.enter_context(sbt("KTF", [128, 2, 128], F32))
                KTT = st.enter_context(sbt("KTT", [128, 128], F32))
                KST = st.enter_context(sbt("KST", [128, 8], F32))
                GKB = st.enter_context(sbt("GKB", [128, 128], F32))
                em.dma('sp', GKB[:], gkb_d[l], writes=[("gkb",)])

                st1 = contextlib.ExitStack()
                GVB = st1.enter_context(sbt("GVB", [128, 2, 1024], F32))
                LNG = st1.enter_context(sbt("LNG", [128, 2, 1024], F32))
                TMP = st1.enter_context(sbt("TMPA", [128, 1024], F32))
                STT = st1.enter_context(sbt("STT", [128, NBLK + 1, 12], F32))
                em.dma('sp', LNG[:, 0, :], lngb_d[l, 0], writes=[("lng",)])
                em.dma('sp', LNG[:, 1, :], lngb_d[l, 1], writes=[("lnb",)])
                wv = [w_acquire('vs', s, l, hold=(s > 0)) for s in range(4)]
                for bi, (c0, nt, smp) in enumerate(blocks):
                    gb_ = bi % 2
                    gk = [("gv", gb_, s) for s in range(4)]
                    for s in range(4):
                        w, wk = wv[s]
                        ps, pk = next_ps()
                        em.pe_group([(ps[0:nt, 0:256], XN[:, k, c0:c0 + nt], w[:, k * 256:(k + 1) * 256],
                                      k == 0, k == ND - 1) for k in range(ND)],
                                    reads=[wk] + xnkeys, writes=[pk])
                        em.op('act', lambda e, gb_=gb_, nt=nt, s=s, ps=ps, bi=bi: e.activation(
                            out=GVB[0:nt, gb_, s * 256:(s + 1) * 256], in_=ps[0:nt, 0:256], func=AF.Gelu_apprx_tanh,
                            accum_out=STT[0:nt, bi, s:s + 1]),
                            reads=[pk], writes=[("gv", gb_, s), ("st", bi, s)])
                    g = GVB[0:nt, gb_, :]
                    stt = STT[0:nt, bi, :]
                    sk_ = [("st", bi, s) for s in range(4)]
                    em.op('act', lambda e, g=g, stt=stt, nt=nt: e.activation(out=TMP[0:nt, :], in_=g, func=AF.Square,
                                                                            accum_out=stt[:, 4:5]),
                          reads=gk, writes=[("tmpa",), ("st", bi)])
                    em.op('dve', lambda e, stt=stt: e.reduce_sum(out=stt[:, 5:6], in_=stt[:, 0:4], axis=AX.X),
                          reads=sk_, writes=[("st5", bi)])
                    em.op('dve', lambda e, stt=stt: e.tensor_scalar(out=stt[:, 6:7], in0=stt[:, 5:6],
                                                                   scalar1=1.0 / 1024, scalar2=None, op0=ALU.mult),
                          reads=[("st5", bi)], writes=[("st6", bi)])
                    em.op('dve', lambda e, stt=stt: e.tensor_tensor(out=stt[:, 7:8], in0=stt[:, 6:7], in1=stt[:, 6:7],
                                                                   op=ALU.mult),
                          reads=[("st6", bi)], writes=[("st7", bi)])
                    em.op('dve', lambda e, stt=stt: e.scalar_tensor_tensor(out=stt[:, 8:9], in0=stt[:, 4:5],
                                                                          scalar=1.0 / 1024, in1=stt[:, 7:8],
                                                                          op0=ALU.mult, op1=ALU.subtract),
                          reads=[("st", bi), ("st7", bi)], writes=[("st8", bi)])
                    em.op('act', lambda e, stt=stt, nt=nt: e.activation(out=stt[:, 9:10], in_=stt[:, 8:9], func=AF.Ln,
                                                                       bias=EPSC[0:nt, 0:1]),
                          reads=[("st8", bi), ("eps",)], writes=[("st9", bi)])
                    em.op('act', lambda e, stt=stt: e.activation(out=stt[:, 10:11], in_=stt[:, 9:10], func=AF.Exp,
                                                                scale=-0.5),
                          reads=[("st9", bi)], writes=[("st10", bi)])
                    em.op('dve', lambda e, g=g, stt=stt: e.tensor_scalar(out=g, in0=g, scalar1=stt[:, 6:7],
                                                                        scalar2=stt[:, 10:11], op0=ALU.subtract,
                                                                        op1=ALU.mult),
                          reads=gk + [("st6", bi), ("st10", bi)], writes=gk)
                    em.op('dve', lambda e, g=g, nt=nt: e.tensor_tensor(out=TMP[0:nt, :], in0=g, in1=LNG[0:nt, 0, :],
                                                                      op=ALU.mult),
                          reads=gk + [("lng",)], writes=[("tmpa",)])
                    if smp:
                        em.op('dve', lambda e, g=g, nt=nt: e.tensor_tensor(out=g, in0=TMP[0:nt, :], in1=LNG[0:nt, 1, :],
                                                                          op=ALU.add),
                              reads=[("tmpa",), ("lnb",)], writes=gk)
                        em.op('act', lambda e, g=g, nt=nt, bi=bi: e.activation(out=VS[0:nt, bi, :], in_=g, func=AF.Copy),
                              reads=gk, writes=[("vs", bi)])
                        em.dma('sp', sgv_d[l], g, reads=gk)
                    else:
                        em.op('dve', lambda e, nt=nt, bi=bi: e.tensor_tensor(out=VS[0:nt, bi, :], in0=TMP[0:nt, :],
                                                                            in1=LNG[0:nt, 1, :], op=ALU.add),
                              reads=[("tmpa",), ("lnb",)], writes=[("vs", bi)])
                ckpt(f"vs_{t}_{l}")

                for s in range(4):
                    w, wk = w_acquire('u', s, l)
                    for mm in range(2):
                        h = 2 * s + mm
                        ps, pk = next_ps()
                        proj_fm(w, wk, mm * 128, ps, pk)
                        em.op('act', lambda e, h=h, ps=ps: e.activation(out=UT[:, h, 0:TT], in_=ps[:, 0:TT],
                                                                       func=AF.Gelu_apprx_tanh),
                              reads=[pk], writes=[("ut", h)])
                for s in range(4):
                    w, wk = w_acquire('q', s, l)
                    for mm in range(2):
                        m = 2 * s + mm
                        ps, pk = next_ps()
                        proj_fm(w, wk, mm * 128, ps, pk)
                        hn_push(headnorm(ps, pk, QT[:, m, 0:TT], ("qt", m), GV_[:, l, G_Q:G_Q + 1]))
                w, wk = w_acquire('kd', 0, l)
                for g in range(2):
                    ps, pk = next_ps()
                    proj_fm(w, wk, g * 128, ps, pk)
                    hn_push(headnorm(ps, pk, KD[:, g, 0:TT], ("kd", g), GV_[:, l, G_K:G_K + 1]))
                hn_flush()
                out_toks = []
                w, wk = w_acquire('kv', 0, l)
                for bi, (c0, nt, smp) in enumerate(blocks):
                    gb = t * NBLK + bi
                    ps, pk = next_ps()
                    em.pe_group([(ps[0:nt, 0:256], XN[:, k, c0:c0 + nt], w[:, k * 256:(k + 1) * 256],
                                  k == 0, k == ND - 1) for k in range(ND)], reads=[wk] + xnkeys, writes=[pk])
                    vsrc = ps[0:nt, 128:256].rearrange("p (g d) -> p g d", g=2)
                    if smp:
                        vdst = VFN[0:nt].rearrange("p (g e) d -> p g e d", g=2)
                        vkey = ("vfn",)
                    else:
                        vdst = VFB[0:nt, bi].rearrange("p (g e) d -> p g e d", g=2)
                        vkey = ("vf", bi)
                    em.op('act', lambda e, vdst=vdst, vsrc=vsrc: e.activation(out=vdst[:, :, 0, 0:64], in_=vsrc, func=AF.Copy),
                          reads=[pk], writes=[vkey])
                    em.op('act', lambda e, vdst=vdst, vsrc=vsrc: e.activation(out=vdst[:, :, 1, 64:128], in_=vsrc, func=AF.Copy),
                          reads=[pk], writes=[vkey])
                    last_prompt = (t == NTILES - 1 and bi == NBLK - 1)
                    if smp or last_prompt:
                        em.op('act', lambda e, nt=nt, ps=ps: e.activation(
                            out=KTF[0:nt].rearrange("p a b -> p (a b)"), in_=ps[0:nt, 0:256], func=AF.Copy),
                            reads=[pk], writes=[("ktf",)])
                        em.op('dve', lambda e, nt=nt: e.tensor_tensor(out=KTT[0:nt, :], in0=KTF[0:nt, 0, :],
                                                                     in1=KTF[0:nt, 0, :], op=ALU.mult),
                              reads=[("ktf",)], writes=[("ktt",)])
                        em.op('dve', lambda e, nt=nt: e.reduce_sum(out=KST[0:nt, 0:2],
                                                                  in_=KTT[0:nt, :].rearrange("p (g d) -> p g d", g=2),
                                                                  axis=AX.X),
                              reads=[("ktt",)], writes=[("kst",)])
                        em.op('act', lambda e, nt=nt: e.activation(out=KST[0:nt, 2:4], in_=KST[0:nt, 0:2], func=AF.Ln,
                                                                  scale=1.0 / HD, bias=EPSC[0:nt, 0:1]),
                              reads=[("kst",), ("eps",)], writes=[("kst",)])
                        em.op('act', lambda e, nt=nt: e.activation(out=KST[0:nt, 4:6], in_=KST[0:nt, 2:4], func=AF.Exp,
                                                                  scale=-0.5),
                              reads=[("kst",)], writes=[("kst",)])
                        for g in range(2):
                            em.op('dve', lambda e, nt=nt, g=g: e.scalar_tensor_tensor(
                                out=KTT[0:nt, g * 64:(g + 1) * 64], in0=KTF[0:nt, 0, g * 64:(g + 1) * 64],
                                scalar=KST[0:nt, 4 + g:5 + g], in1=GKB[0:nt, g * 64:(g + 1) * 64],
                                op0=ALU.mult, op1=ALU.mult),
                                reads=[("ktf",), ("kst",), ("gkb",)], writes=[("ktt",)])
                        if last_prompt:
                            out_toks.append(em.dma('sp', kwp_d[l], KTT[:, :], reads=[("ktt",)]))
                            out_toks.append(em.dma('sp', vwp_d[l], KTF[:, 1, :], reads=[("ktf",)]))
                        else:
                            for b in range(NSQ):
                                out_toks.append(em.dma('sp', kws_d[l, b, 124:128, :], KTT[4 * b:4 * b + 4, :], reads=[("ktt",)]))
                                out_toks.append(em.dma('sp', vws_d[l, b, 124:128, :], KTF[4 * b:4 * b + 4, 1, :], reads=[("ktf",)]))
                            out_toks.append(em.dma('sp', kws_d[l, :, 0:124, :], ck_d[l, :, 4:128, :]))
                            out_toks.append(em.dma('sp', vws_d[l, :, 0:124, :], cv_d[l, :, 4:128, :]))

                em.barrier(exclude=out_toks)
                st1.close()
                st2 = contextlib.ExitStack()
                EB = st2.enter_context(sbt("EB", [128, 1024], F32))
                PT = st2.enter_context(sbt("PT", [128, 3, 1024], BF16))
                if has_s:
                    KSTG = st2.enter_context(sbt("KSTG", [128, 2, 2, 128], F32))
                    VSTG = st2.enter_context(sbt("VSTG", [128, 2, 128], F32))
                    EBS = st2.enter_context(sbt("EBS", [128, 4, 64], F32))
                    PTS = st2.enter_context(sbt("PTS", [128, 4, 64], BF16))
                    DRS = st2.enter_context(sbt("DRS", [128, 4, 16], F32))
                ATT = st2.enter_context(sbt("ATT", [128, 8, 128], F32))
                SGO = st2.enter_context(sbt("SGO", [128, 8, 128], F32))
                SQB = st2.enter_context(sbt("SQB", [128, 16, 128], BF16))
                DR = st2.enter_context(sbt("DR", [128, 256], F32))
                ckpt(f"proj_{t}_{l}")
                pt_i = [0, 0]

                def attn_unit(m0, nm, g, c0, nq, prev, cur, mprev, mcur, att_out, small=False):
                    pss, pks = next_ps()
                    F = nm * nq
                    W = 2 * F
                    S4 = pss[:, 0:1024].rearrange("p (e x) -> p e x", e=2)[:, :, 0:W].rearrange(
                        "p e (c x) -> p e c x", c=2)
                    mms = []
                    rk = [("qt", m0 + i) for i in range(nm)]
                    for e_ in range(2):
                        q_ap = QT[64 * e_:64 * e_ + 64, m0:m0 + nm, c0:c0 + nq]
                        if prev is not None:
                            mms.append((S4[0:prev[2], e_, 0, :], prev[0](g, e_), q_ap, True, True))
                        mms.append((S4[0:cur[2], e_, 1, :], cur[0](g, e_), q_ap, True, True))
                    if prev is not None:
                        rk = rk + prev[3]
                    rk = rk + cur[3]
                    em.pe_group(mms, reads=rk, writes=[pks])
                    if small:
                        pi = pt_i[1] % 4
                        pt_i[1] += 1
                        ebuf, pbuf, dbuf = EBS[:, pi, :], PTS[:, pi, :], DRS[:, pi, :]
                        ekey, pkey, dkey, meng = ("ebs", pi), ("pts", pi), ("drs", pi), 'dve'
                    else:
                        pi = pt_i[0] % 3
                        pt_i[0] += 1
                        ebuf, pbuf, dbuf = EB[:, :], PT[:, pi, :], DR[:, :]
                        ekey, pkey, dkey, meng = ("eb",), ("pt", pi), ("dr",), 'pool'
                    E4 = ebuf[:, 0:2 * W].rearrange("p (e c x) -> p e c x", e=2, c=2)
                    P4 = pbuf[:, 0:2 * W].rearrange("p (e c x) -> p e c x", e=2, c=2)
                    parts = ([(0, prev[2], mprev)] if prev is not None else []) + [(1, cur[2], mcur)]
                    for (ch, nk, mk) in parts:
                        em.op('act', lambda e, ch=ch, nk=nk: e.activation(out=E4[0:nk, :, ch, :], in_=S4[0:nk, :, ch, :],
                                                                         func=AF.Exp, scale=0.125),
                              reads=[pks], writes=[ekey + (ch,)])
                        em.op(meng if ch == 0 else 'dve', lambda e, ch=ch, nk=nk, mk=mk: e.tensor_tensor(
                            out=P4[0:nk, :, ch, :], in0=E4[0:nk, :, ch, :], in1=mk, op=ALU.mult),
                            reads=[ekey + (ch,), ("maskp",), ("masks",)], writes=[pkey + (ch,)])
                    yield
                    pso, pko = next_ps()
                    O3 = pso[:, 0:W].rearrange("p (o x) -> p o x", o=2)
                    mms = []
                    for od in range(2):
                        lst = []
                        for e_ in range(2):
                            for (ch, nk, mk) in parts:
                                src = prev if ch == 0 else cur
                                lhs = src[1](g, e_) if od == 0 else CMAT[0:nk, 2 + e_, :]
                                lst.append((lhs, P4[0:nk, e_, ch, :]))
                        for i, (lhs, rhs) in enumerate(lst):
                            mms.append((O3[:, od, :], lhs, rhs, i == 0, (od == 0 and i == len(lst) - 1)))
                        if od == 1:
                            for mi in range(nm):
                                mms.append((O3[:, 1, mi * nq:(mi + 1) * nq], BSR[64:66, l, m0 + mi, :],
                                            CMAT[64:66, 0, 0:nq], False, mi == nm - 1))
                    rk = [pkey + (1,), ("cmat",), ("esr",)] + cur[4]
                    if prev is not None:
                        rk += [pkey + (0,)] + prev[4]
                    em.pe_group(mms, reads=rk, writes=[pko])
                    DRf = dbuf[:, 0:F]
                    em.op('act', lambda e: e.activation(out=DRf, in_=O3[:, 1, :], func=AF.Ln),
                          reads=[pko], writes=[dkey])
                    em.op('act', lambda e: e.activation(out=DRf, in_=DRf, func=AF.Exp, scale=-1.0),
                          reads=[dkey], writes=[dkey])
                    em.op('dve', lambda e: e.tensor_tensor(out=att_out,
                                                           in0=O3[:, 0, :].rearrange("p (m q) -> p m q", m=nm),
                                                           in1=DRf.rearrange("p (m q) -> p m q", m=nm), op=ALU.mult),
                          reads=[pko, dkey], writes=[("att", m0 + i) for i in range(nm)])

                def sgu_part(c0, nq, bi, smp):
                    psg, pkg = next_ps()
                    G3 = psg[:, 0:8 * nq].rearrange("p (h q) -> p h q", h=8)
                    mms = []
                    for h in range(8):
                        if smp:
                            wmat = WSS[0:nq, l, h, 0:nq]
                            brow = BSR[32:33, l, h, 0:nq]
                            orow = CMAT[32:33, 0, :]
                        else:
                            wmat = WST[0:nq, l, h, 0:nq]
                            brow = BSR[0:1, l, h, 0:nq]
                            orow = CMAT[0:1, 0, :]
                        mms.append((G3[:, h, :], VS[0:nq, bi, h * 128:(h + 1) * 128], wmat, True, False))
                        mms.append((G3[:, h, :], orow, brow, False, True))
                    em.pe_group(mms, reads=[("vs", bi), ("wst", l), ("wss", l), ("bsr", l), ("bss", l), ("cmat",)],
                                writes=[pkg])
                    em.op('dve', lambda e: e.tensor_tensor(out=SGO[:, :, 0:nq], in0=UT[:, :, c0:c0 + nq], in1=G3,
                                                           op=ALU.mult),
                          reads=[pkg] + [("ut", h) for h in range(8)], writes=[("sgo",)])

                def merge_part(c0, nq):
                    em.op('act', lambda e: e.activation(out=SQB[:, 0:8, 0:nq], in_=ATT[:, :, 0:nq], func=AF.Square),
                          reads=[("att", m) for m in range(8)], writes=[("sqb", 0)])
                    em.op('act', lambda e: e.activation(out=SQB[:, 8:16, 0:nq], in_=SGO[:, :, 0:nq], func=AF.Square),
                          reads=[("sgo",)], writes=[("sqb", 1)])
                    psn, pkn = next_ps()
                    N2 = psn[:, 0:2 * nq].rearrange("p (a q) -> p a q", a=2)
                    mms = []
                    for a in range(2):
                        for m in range(8):
                            mms.append((N2[:, a, :], ONES, SQB[:, 8 * a + m, 0:nq], m == 0, m == 7))
                    em.pe_group(mms, reads=[("sqb", 0), ("sqb", 1), ("cmat",)], writes=[pkn])
                    r = rstd_from_ps(psn[:, 0:2 * nq], pkn, 2 * nq, 1.0 / 1024, 1)
                    R2 = r.rearrange("p (a q) -> p a q", a=2)
                    Gb = lambda off: GV_[:, l, off:off + 8].rearrange("p (m o) -> p m o", o=1).broadcast_to([128, 8, nq])
                    em.op('dve', lambda e: e.tensor_tensor(out=ATT[:, :, 0:nq], in0=ATT[:, :, 0:nq], in1=Gb(G_OA), op=ALU.mult),
                          reads=[("att", m) for m in range(8)] + [("gvec",), ("sqb", 0)],
                          writes=[("att", m) for m in range(8)])
                    em.op('dve', lambda e: e.tensor_tensor(out=SGO[:, :, 0:nq], in0=SGO[:, :, 0:nq], in1=Gb(G_OS), op=ALU.mult),
                          reads=[("sgo",), ("gvec",), ("sqb", 1)], writes=[("sgo",)])
                    Rb = lambda a: r[:, a * nq:(a + 1) * nq].rearrange("p (o q) -> p o q", o=1).broadcast_to([128, 8, nq])
                    em.op('dve', lambda e: e.tensor_tensor(out=XN[:, 0:8, c0:c0 + nq], in0=ATT[:, :, 0:nq], in1=Rb(0),
                                                           op=ALU.mult),
                          reads=[("att", m) for m in range(8)] + [("rs", 1)], writes=[("xn", m) for m in range(8)])
                    em.op('dve', lambda e: e.tensor_tensor(out=XN[:, 8:16, c0:c0 + nq], in0=SGO[:, :, 0:nq], in1=Rb(1),
                                                           op=ALU.mult),
                          reads=[("sgo",), ("rs", 1)], writes=[("xn", 8 + m) for m in range(8)])

                def with_prelude(pre, gen):
                    pre()
                    yield from gen

                items = []
                for bi, (c0, nt, smp) in enumerate(blocks):
                    if not smp:
                        gb = t * NBLK + bi
                        if gb == 0:
                            prev = None
                        else:
                            if bi == 0:
                                kp = lambda g, e_: KDC[64 * e_:64 * e_ + 64, l, g, :]
                                kpk = [("kdc", l)]
                                vp = lambda g, e_: VFC[:, l, 2 * g + e_, :]
                                vpk = [("vfc", l), ("vfcz",)]
                            else:
                                kp = lambda g, e_, c0=c0: KD[64 * e_:64 * e_ + 64, g, c0 - 128:c0]
                                kpk = [("kd", 0), ("kd", 1)]
                                vp = lambda g, e_, bi=bi: VFB[:, bi - 1, 2 * g + e_, :]
                                vpk = [("vf", bi - 1), ("vfz",)]
                            prev = (kp, vp, 128, kpk, vpk)
                        kc_ = lambda g, e_, c0=c0: KD[64 * e_:64 * e_ + 64, g, c0:c0 + 128]
                        vc = lambda g, e_, bi=bi: VFB[:, bi, 2 * g + e_, :]
                        cur = (kc_, vc, 128, [("kd", 0), ("kd", 1)], [("vf", bi), ("vfz",)])
                        for mp in range(4):
                            gen = attn_unit(2 * mp, 2, mp // 2, c0, 128, prev, cur,
                                            MASKP[:, mp, :, 0, :], MASKP[:, mp, :, 1, :],
                                            ATT[:, 2 * mp:2 * mp + 2, :])
                            post = None
                            if mp == 3:
                                nxt = blocks[bi + 1] if bi + 1 < len(blocks) else None

                                def post(c0=c0, bi=bi, nxt=nxt):
                                    merge_part(c0, 128)
                                    if nxt is not None:
                                        sgu_part(nxt[0], nxt[1], bi + 1, nxt[2])
                            items.append((gen, post, 2))
                    else:
                        for b in range(NSQ):
                            ri = b % 2

                            def pre(b=b, ri=ri, c0=c0, bi=bi):
                                for e_ in range(2):
                                    em.dma('sp', KSTG[64 * e_:64 * e_ + 64, ri, :, :],
                                           ckT_d[l, b].rearrange("g d k -> d g k"), writes=[("kstg", ri)])
                                em.dma('sp', VSTG[:, ri, :], cv_d[l, b], writes=[("vstg", ri)])
                                em.op('dve', lambda e: e.tensor_copy(out=KDS[:, ri, :, :], in_=KSTG[:, ri, :, :]),
                                      reads=[("kstg", ri)], writes=[("kds", ri)])
                                vd = VFS[:, ri].rearrange("p (g e) d -> p g e d", g=2)
                                vsrc = VSTG[:, ri, :].rearrange("p (g d) -> p g d", g=2)
                                em.op('dve', lambda e: e.tensor_copy(out=vd[:, :, 0, 0:64], in_=vsrc),
                                      reads=[("vstg", ri)], writes=[("vfs", ri)])
                                em.op('dve', lambda e: e.tensor_copy(out=vd[:, :, 1, 64:128], in_=vsrc),
                                      reads=[("vstg", ri)], writes=[("vfs", ri)])

                            kp = lambda g, e_, ri=ri: KDS[64 * e_:64 * e_ + 64, ri, g, :]
                            vp = lambda g, e_, ri=ri: VFS[:, ri, 2 * g + e_, :]
                            prev = (kp, vp, 128, [("kds", ri)], [("vfs", ri), ("vfsz",)])
                            kc_ = lambda g, e_: KD[64 * e_:64 * e_ + 64, g, TPT:TPT + NSAMP]
                            vc = lambda g, e_: VFN[:, 2 * g + e_, :]
                            cur = (kc_, vc, NSAMP, [("kd", 0), ("kd", 1)], [("vfn",), ("vfnz",)])
                            q0 = c0 + NST * b
                            for g in range(2):
                                gen = attn_unit(4 * g, 4, g, q0, NST, prev, cur,
                                                MASKSP[:, g, :, :], MASKS[:, b, g, :, :],
                                                ATT[:, 4 * g:4 * g + 4, NST * b:NST * b + NST], small=True)
                                if g == 0:
                                    gen = with_prelude(pre, gen)
                                post = (lambda c0=c0: merge_part(c0, NSAMP)) if (g == 1 and b == NSQ - 1) else None
                                items.append((gen, post, 3))
                if items:
                    sgu_part(blocks[0][0], blocks[0][1], 0, blocks[0][2])
                pend = []

                def finish_one():
                    gen, post, _ = pend.pop(0)
                    for _ in gen:
                        pass
                    if post is not None:
                        post()

                for it in items:
                    while len(pend) > it[2] - 1:
                        finish_one()
                    next(it[0])
                    pend.append(it)
                while pend:
                    finish_one()
                if t < NTILES - 1:
                    em.op('dve', lambda e: e.tensor_copy(out=KDC[:, l, :, :], in_=KD[:, :, TPT - 128:TPT]),
                          reads=[("kd", 0), ("kd", 1)], writes=[("kdc", l)])
                    em.op('dve', lambda e: e.tensor_copy(out=VFC[:, l, :, :], in_=VFB[:, NBLK - 1, :, :]),
                          reads=[("vf", NBLK - 1), ("vfz",), ("vfcz",)], writes=[("vfc", l)])

                ckpt(f"attn_{t}_{l}")
                for s in range(8):
                    w, wk = w_acquire('o', s, l)
                    if s == 0:
                        stats_begin()
                    for mm in range(2):
                        m = 2 * s + mm
                        ps, pk = next_ps()
                        proj_fm(w, wk, mm * 128, ps, pk)
                        em.op('dve', lambda e, m=m, ps=ps: e.tensor_tensor(out=XT[:, m, 0:TT], in0=XT[:, m, 0:TT],
                                                                          in1=ps[:, 0:TT], op=ALU.add),
                              reads=[pk, ("xt", m)], writes=[("xt", m)])
                        stats_push(m)
                stats_finish(l, G_FFN)
                em.barrier()
                st2.close()
            ckpt(f"mixer_{t}_{l}")
            with contextlib.ExitStack() as st:
                HB = st.enter_context(sbt("HB", [128, 2, 2, 2 + TPT], F32))
                T0, TA = ZQ, RS
                T1 = st.enter_context(sbt("T1", [128, 2, TTMAX], F32))
                SGB = st.enter_context(sbt("SGB", [128, 2, TTMAX], F32))
                ACTB = st.enter_context(sbt("ACTB", [128, 15, TTMAX], BF16))
                if has_s:
                    CS = st.enter_context(sbt("CS", [128, 2 * NFC, NSQ, 2], F32))
                    HS = st.enter_context(sbt("HS", [128, 2, NSQ, 6], F32))
                    em.dma('sp', CS[:].rearrange("p c b j -> p (c b j)"), cs_d[l].rearrange("p c b j -> p (c b j)"),
                           writes=[("cs",)])
                for g, (f0, f1) in enumerate(FG):
                    nk = f1 - f0
                    for j in range(f0, f1):
                        w, wk = w_acquire('up', j, l)
                        rj = j % 2
                        for half in range(2):
                            cidx = j + NFC * half
                            ps, pk = next_ps()
                            proj_fm(w, wk, half * 128, ps, pk)
                            hb = HB[:, rj, half, :]
                            hk = ("hb", rj, half)
                            cw = lambda i, cidx=cidx: CWB[:, l, cidx, i:i + 1]
                            em.op('act', lambda e, hb=hb, ps=ps: e.activation(out=hb[:, 2:2 + TPT], in_=ps[:, 0:TPT], func=AF.Copy),
                                  reads=[pk], writes=[hk])
                            em.op('pool', lambda e, hb=hb, cidx=cidx: e.tensor_copy(out=hb[:, 0:2], in_=CONVC[:, l, cidx, :]),
                                  reads=[("convc",)], writes=[hk])
                            em.op('pool', lambda e, hb=hb, cidx=cidx: e.tensor_copy(out=CONVC[:, l, cidx, :], in_=hb[:, TPT:TPT + 2]),
                                  reads=[hk], writes=[("convc",)])
                            t0 = T0[:, half, :]
                            t1 = T1[:, half, :]
                            tfin = TA[:, rj, :] if half == 0 else T0[:, half, :]
                            em.op('act', lambda e, hb=hb, t0=t0, cw=cw: e.activation(
                                out=t0[:, 0:TPT], in_=hb[:, 0:TPT], func=AF.Identity, scale=cw(0), bias=cw(3)),
                                reads=[hk, ("cwb",)], writes=[("zq", half)])
                            em.op('dve', lambda e, hb=hb, t0=t0, t1=t1, cw=cw: e.scalar_tensor_tensor(
                                out=t1[:, 0:TPT], in0=hb[:, 1:1 + TPT], scalar=cw(1), in1=t0[:, 0:TPT],
                                op0=ALU.mult, op1=ALU.add),
                                reads=[hk, ("zq", half), ("cwb",)], writes=[("rs0", half)])
                            fkey = ("rs", rj) if half == 0 else ("zq", half)
                            em.op('dve', lambda e, hb=hb, t1=t1, tfin=tfin, cw=cw: e.scalar_tensor_tensor(
                                out=tfin[:, 0:TPT], in0=hb[:, 2:2 + TPT], scalar=cw(2), in1=t1[:, 0:TPT],
                                op0=ALU.mult, op1=ALU.add),
                                reads=[hk, ("rs0", half), ("cwb",)], writes=[fkey])
                            if has_s:
                                hs = HS[:, half, :, :]
                                sk = ("hs", half)
                                em.op('act', lambda e, hs=hs, ps=ps: e.activation(
                                    out=hs[:, :, 2:6], in_=ps[:, TPT:TT].rearrange("p (b t) -> p b t", b=NSQ), func=AF.Copy),
                                    reads=[pk], writes=[sk])
                                em.op('pool', lambda e, hs=hs, cidx=cidx: e.tensor_copy(out=hs[:, :, 0:2], in_=CS[:, cidx, :, :]),
                                      reads=[("cs",)], writes=[sk])
                                em.op('pool', lambda e, hs=hs, cidx=cidx: e.tensor_copy(out=CS[:, cidx, :, :], in_=hs[:, :, 4:6]),
                                      reads=[sk], writes=[("cs",)])
                                v3 = lambda ap: ap[:, TPT:TT].rearrange("p (b t) -> p b t", b=NSQ)
                                em.op('act', lambda e, hs=hs, t0=t0, cw=cw, v3=v3: e.activation(
                                    out=v3(t0), in_=hs[:, :, 0:4], func=AF.Identity, scale=cw(0), bias=cw(3)),
                                    reads=[sk, ("cwb",)], writes=[("zq", half)])
                                em.op('dve', lambda e, hs=hs, t0=t0, t1=t1, cw=cw, v3=v3: e.scalar_tensor_tensor(
                                    out=v3(t1), in0=hs[:, :, 1:5], scalar=cw(1), in1=v3(t0), op0=ALU.mult, op1=ALU.add),
                                    reads=[sk, ("zq", half), ("cwb",)], writes=[("rs0", half)])
                                em.op('dve', lambda e, hs=hs, t1=t1, tfin=tfin, cw=cw, v3=v3: e.scalar_tensor_tensor(
                                    out=v3(tfin), in0=hs[:, :, 2:6], scalar=cw(2), in1=v3(t1), op0=ALU.mult, op1=ALU.add),
                                    reads=[sk, ("rs0", half), ("cwb",)], writes=[fkey])
                            if half == 1:
                                em.op('act', lambda e, tfin=tfin, rj=rj: e.activation(out=SGB[:, rj, 0:TT], in_=tfin[:, 0:TT],
                                                                                     func=AF.Silu),
                                      reads=[fkey], writes=[("sgb", rj)])
                        em.op('dve', lambda e, rj=rj, j=j, f0=f0: e.tensor_tensor(
                            out=ACTB[:, j - f0, 0:TT], in0=TA[:, rj, 0:TT], in1=SGB[:, rj, 0:TT], op=ALU.mult),
                            reads=[("rs", rj), ("sgb", rj)], writes=[("actb", j - f0)])
                    akeys = [("actb", i) for i in range(nk)]
                    for m in range(16):
                        w, wk = w_acquire('dn', (g, m), l)
                        lastg = (g == len(FG) - 1)
                        ovl = (l == 0 and lastg)
                        nxt = (l == DEPTH - 1 and lastg and t < NTILES - 1)
                        if ovl and m == 0:
                            stats_begin()
                        if nxt and m == 0:
                            TTn = TTMAX if t + 1 == NTILES - 1 else TPT
                            stats_begin(TTn, [(0, 512)] + ([(512, NSAMP)] if t + 1 == NTILES - 1 else []))
                        ps, pk = next_ps()
                        proj_fm(w, wk, 0, ps, pk, kc=nk, rhs=ACTB, rkeys=akeys, stride=128, split_last=(m == 0))
                        em.op('dve', lambda e, m=m, ps=ps: e.tensor_tensor(out=XT[:, m, 0:TT], in0=XT[:, m, 0:TT],
                                                                          in1=ps[:, 0:TT], op=ALU.add),
                              reads=[pk, ("xt", m)], writes=[("xt", m)])
                        if ovl:
                            stats_push(m)
                        if nxt:
                            em.dma('sp', yT_d[:, m, t * TPT:(t + 1) * TPT], XT[:, m, 0:TPT], reads=[("xt", m)])
                            for mm_ in ([m - 4] if m >= 4 else []) + (list(range(m - 3, m + 1)) if m == 15 else []):
                                em.dma('sp', XT[:, mm_, 0:TPT], xT_d[:, mm_, (t + 1) * TPT:(t + 2) * TPT],
                                       writes=[("xt", mm_)])
                                stats_push(mm_, delay=4)
                if has_s:
                    em.dma('sp', cvp_d[l].rearrange("p c j -> p (c j)"), CONVC[:, l].rearrange("p c j -> p (c j)"),
                           reads=[("convc",)])
                    em.dma('sp', cvs_d[l].rearrange("p c b j -> p (c b j)"), CS[:].rearrange("p c b j -> p (c b j)"),
                           reads=[("cs",)])
                em.barrier()
            ckpt(f"ffn_{t}_{l}")
        if t == NTILES - 1:
            em.dma('sp', yT_d[:, :, t * TPT:(t + 1) * TPT], XT[:, :, 0:TPT], reads=xkeys)
            em.dma('sp', yT_d[:, :, S:S + NSAMP], XT[:, :, TPT:TT], reads=xkeys)

    try:
        ckpt("consts")
        for t in range(NTILES):
            run_tile(t)
    except StopBuild:
        pass
    em.finish()
    return nc


def _arr_cols(W, cols):
    Wc = W[:, cols]
    kc = W.shape[0] // 128
    return Wc.reshape(kc, 128, Wc.shape[1]).transpose(1, 0, 2).reshape(128, kc * Wc.shape[1])


def _prep_weights(w_in, w_o, w_up, w_down):
    out = np.empty((DEPTH, 128, EPL), np.float32)
    ar = np.arange
    for l in range(DEPTH):
        for (kind, idx, off, ne) in SLOTS:
            if kind == 'vs':
                a = _arr_cols(w_in[l], ar(2304 + 256 * idx, 2304 + 256 * idx + 256))
            elif kind == 'u':
                a = _arr_cols(w_in[l], ar(1280 + 256 * idx, 1280 + 256 * idx + 256))
            elif kind == 'q':
                a = _arr_cols(w_in[l], ar(256 * idx, 256 * idx + 256))
            elif kind == 'kd':
                k0 = ar(1024, 1088)
                k1 = ar(1088, 1152)
                a = _arr_cols(w_in[l], np.concatenate([k0, k0, k1, k1]))
            elif kind == 'kv':
                a = _arr_cols(w_in[l], ar(1024, 1280))
            elif kind == 'o':
                a = _arr_cols(w_o[l], ar(256 * idx, 256 * idx + 256))
            elif kind == 'up':
                a = _arr_cols(w_up[l], np.concatenate([ar(128 * idx, 128 * idx + 128),
                                                       ar(DFF + 128 * idx, DFF + 128 * idx + 128)]))
            elif kind == 'dn':
                g, m = idx
                f0, f1 = FG[g]
                a = _arr_cols(w_down[l][f0 * 128:f1 * 128], ar(128 * m, 128 * m + 128))
            out[l, :, off:off + ne] = a
    return out


def _consts():
    h = np.arange(1, NH + 1, dtype=np.float32)
    slopes = np.exp2(-8.0 * h / NH).astype(np.float32)
    j = np.arange(128)[:, None]
    i = np.arange(128)[None, :]
    maskp = np.zeros((128, 4, 2, 2, 2, 128), np.float32)
    masks = np.zeros((NSAMP, NSQ, 2, 2, 4, NST), np.float32)
    masksp = np.zeros((128, 2, 2, 4, NST), np.float32)
    dprev = np.maximum(i - j + 128, 0).astype(np.float32)
    dcur = np.maximum(i - j, 0).astype(np.float32)
    for mp in range(4):
        for e_ in range(2):
            for mi in range(2):
                sl = slopes[4 * mp + 2 * mi + e_]
                maskp[:, mp, e_, 0, mi, :] = np.where(j > i, np.exp(-sl * dprev), 0.0)
                maskp[:, mp, e_, 1, mi, :] = np.where(j <= i, np.exp(-sl * dcur), 0.0)
                gg, ml = mp // 2, 2 * (mp % 2) + mi
                masksp[:, gg, e_, ml, :] = maskp[:, mp, e_, 0, mi, 0:NST]
                for jj in range(NSAMP):
                    b, tj = jj // NST, jj % NST
                    for ti in range(NST):
                        if tj <= ti:
                            masks[jj, b, gg, e_, ml, ti] = np.exp(-sl * float(ti - tj))
    maskp = maskp.reshape(128, 4, 2, 2, 256)
    masks = masks.reshape(NSAMP, NSQ, 2, 2, 4 * NST)
    masksp = masksp.reshape(128, 2, 2, 4 * NST)
    tril = (j <= i).astype(np.float32)
    js = np.arange(NSAMP)[:, None]
    is_ = np.arange(NSAMP)[None, :]
    trils = ((js // NST == is_ // NST) & (js % NST <= is_ % NST)).astype(np.float32)
    cmat = np.zeros((128, 4, 128), np.float32)
    cmat[:, 0, :] = 1.0
    cmat[:, 1, :] = ((np.arange(128)[:, None] // 64) == (np.arange(128)[None, :] // 64)).astype(np.float32)
    cmat[:, 2, 0:64] = 1.0
    cmat[:, 3, 64:128] = 1.0
    return maskp, masks, masksp, tril, trils, cmat


_NC_CACHE = {}
_DEBUG = {}


def kernel(x_prompt, x_sample, cache_k_win, cache_v_win, state_ffn_conv,
           norm_mix_g, w_in, q_norm_g, k_norm_g, attn_sinks, sgu_ln_g, sgu_ln_b,
           sgu_w, sgu_b, out_norm_att_g, out_norm_sgu_g, w_o, norm_ffn_g,
           w_up, conv_w, conv_b, w_down):
    f32 = np.float32
    A = lambda a: np.asarray(a, dtype=f32)
    x_prompt, x_sample = A(x_prompt), A(x_sample)
    cache_k_win, cache_v_win, state_ffn_conv = A(cache_k_win), A(cache_v_win), A(state_ffn_conv)
    w_in, w_o, w_up, w_down = A(w_in), A(w_o), A(w_up), A(w_down)

    wts = _prep_weights(w_in, w_o, w_up, w_down)
    maskp, masks, masksp, tril, trils, cmat = _consts()

    def cvec(v, n):
        return A(v).reshape(DEPTH, n, 128).transpose(2, 0, 1)

    gvec = np.zeros((128, DEPTH, 59), f32)
    gvec[64, :, 58] = 1.0
    gvec[:, :, 0:16] = cvec(norm_mix_g, 16)
    gvec[:, :, 16:32] = cvec(norm_ffn_g, 16)
    gvec[:, :, 32:40] = cvec(out_norm_att_g, 8)
    gvec[:, :, 40:48] = cvec(out_norm_sgu_g, 8)
    sk = A(attn_sinks)
    p = np.arange(128)
    for m in range(8):
        gvec[:, :, 48 + m] = sk[:, 2 * m + p // 64].T
    skrow = np.zeros((DEPTH, 8, 128), f32)
    for m in range(8):
        skrow[:, m, :] = sk[:, 2 * m + p // 64]
    skrow = skrow.reshape(1, DEPTH * 8 * 128)
    gvec[:, :, 56] = A(q_norm_g)[:, p % 64].T
    gvec[:, :, 57] = A(k_norm_g)[:, p % 64].T
    cwb = np.zeros((128, DEPTH, 2 * NFC, 4), f32)
    cwv = A(conv_w).reshape(DEPTH, 3, 2 * NFC, 128)
    cwb[:, :, :, 0:3] = cwv.transpose(3, 0, 2, 1)
    cwb[:, :, :, 3] = A(conv_b).reshape(DEPTH, 2 * NFC, 128).transpose(2, 0, 1)
    lngb = np.zeros((DEPTH, 2, 128, 1024), f32)
    lngb[:, 0] = A(sgu_ln_g)[:, None, :]
    lngb[:, 1] = A(sgu_ln_b)[:, None, :]
    gkb = np.broadcast_to(np.tile(A(k_norm_g), (1, 2))[:, None, :], (DEPTH, 128, 128)).copy()
    wsT = np.ascontiguousarray(A(sgu_w).transpose(0, 3, 1, 2))
    wsS = np.zeros((DEPTH, NSAMP, 8, NSAMP), f32)
    for b in range(NSQ):
        wsS[:, NST * b:NST * b + NST, :, NST * b:NST * b + NST] = wsT[:, 0:NST, :, 0:NST]
    bsr = np.ascontiguousarray(A(sgu_b))
    bss = np.ascontiguousarray(np.tile(A(sgu_b)[:, :, 0:NST], (1, 1, NSQ)))

    in_maps = []
    for c in range(NCORES):
        xfull = np.concatenate([x_prompt[c], x_sample[NSQ * c:NSQ * (c + 1)].reshape(NSAMP, D)], axis=0)
        xT = np.ascontiguousarray(xfull.reshape(NTOK, ND, 128).transpose(2, 1, 0))
        ck = cache_k_win[:, NSQ * c:NSQ * (c + 1)]
        cv = cache_v_win[:, NSQ * c:NSQ * (c + 1)]
        ckT = np.ascontiguousarray(ck.transpose(0, 1, 3, 4, 2))
        cs = state_ffn_conv[:, NSQ * c:NSQ * (c + 1)]
        cs_l = np.ascontiguousarray(cs.reshape(DEPTH, NSQ, 2, 2 * NFC, 128).transpose(0, 4, 3, 1, 2))
        in_maps.append({
            "xT": xT, "wts": wts, "ckT": ckT,
            "ck": np.ascontiguousarray(ck.reshape(DEPTH, NSQ, 128, 128)),
            "cv": np.ascontiguousarray(cv.reshape(DEPTH, NSQ, 128, 128)),
            "cs": cs_l, "gvec": gvec, "skrow": skrow, "cwb": cwb, "lngb": lngb, "gkb": gkb, "wsT": wsT, "wsS": wsS,
            "bsr": bsr, "bss": bss, "maskp": maskp, "masks": masks, "masksp": masksp, "tril": tril, "trils": trils, "cmat": cmat,
        })

    if _DEBUG.get("prep_only"):
        return in_maps
    if "nc" not in _NC_CACHE:
        _NC_CACHE["nc"] = build_program()
    nc = _NC_CACHE["nc"]
    res = run_bass_kernel_spmd(nc, in_maps, core_ids=list(range(NCORES)))
    R = res.results

    y_prompt = np.empty((NCORES, S, D), f32)
    y_sample = np.empty((NCORES * NSQ, NST, D), f32)
    kwp = np.empty((DEPTH, NCORES, 128, 2, HD), f32)
    vwp = np.empty_like(kwp)
    cvp = np.empty((DEPTH, NCORES, 2, 2 * DFF), f32)
    kws = np.empty((DEPTH, NCORES * NSQ, 128, 2, HD), f32)
    vws = np.empty_like(kws)
    cvs = np.empty((DEPTH, NCORES * NSQ, 2, 2 * DFF), f32)
    sgv = np.empty((DEPTH, NCORES * NSQ, NST, 8, 128), f32)
    for c in range(NCORES):
        r = R[c]
        y = np.asarray(r["yT"]).transpose(2, 1, 0).reshape(NTOK, D)
        y_prompt[c] = y[:S]
        y_sample[NSQ * c:NSQ * (c + 1)] = y[S:].reshape(NSQ, NST, D)
        kwp[:, c] = np.asarray(r["kwp"]).reshape(DEPTH, 128, 2, HD)
        vwp[:, c] = np.asarray(r["vwp"]).reshape(DEPTH, 128, 2, HD)
        cvp[:, c] = np.asarray(r["cvp"]).transpose(0, 3, 2, 1).reshape(DEPTH, 2, 2 * DFF)
        kws[:, NSQ * c:NSQ * (c + 1)] = np.asarray(r["kws"]).reshape(DEPTH, NSQ, 128, 2, HD)
        vws[:, NSQ * c:NSQ * (c + 1)] = np.asarray(r["vws"]).reshape(DEPTH, NSQ, 128, 2, HD)
        cvs[:, NSQ * c:NSQ * (c + 1)] = np.asarray(r["cvs"]).transpose(0, 3, 4, 2, 1).reshape(DEPTH, NSQ, 2, 2 * DFF)
        sgv[:, NSQ * c:NSQ * (c + 1)] = np.asarray(r["sgv"]).reshape(DEPTH, NSQ, NST, 8, 128)
    return (y_prompt, y_sample, kwp, vwp, cvp, kws, vws, cvs, sgv)
```
